# Optimizing a Trainium2 kernel written in Bass

```python
import math
import jax, jax.numpy as jnp
from jax import lax
import numpy as np

D_MODEL = 1024
BATCH = 8
SEQ = 4096
DEPTH = 4

GRID_W = 64
CTX_LEN = 256

RWKV_HEADS = 8
RWKV_HEAD_DIM = 64
RWKV_WIDTH = RWKV_HEADS * RWKV_HEAD_DIM
DECAY_LORA = 64
AAA_LORA = 64
GN_EPS = 64e-5
NA_HEADS = 8
NA_HEAD_DIM = 64
NA_WIDTH = NA_HEADS * NA_HEAD_DIM
NA_WIN_R = 8
NA_WIN_C = 16
DIFF_HEADS = 4
DIFF_QK_DIM = 64
DIFF_V_DIM = 2 * DIFF_QK_DIM
DIFF_WIDTH = DIFF_HEADS * DIFF_V_DIM
Q_BLOCK = 128
ROPE_THETA = 10000.0
SUBLN_EPS = 1e-5
N_BRANCH = 3
BRANCH_WIDTH = 512
RMS_EPS = 1e-6
NEG_INF = -1e30

RWKV_SHIFT_WIDTH = 3 * RWKV_WIDTH + 2 * DECAY_LORA + 2 * AAA_LORA
DIFF_QKV_WIDTH = 2 * (DIFF_HEADS * 2 * DIFF_QK_DIM) + DIFF_WIDTH
IN_SIZES = (RWKV_SHIFT_WIDTH, RWKV_WIDTH,
            3 * NA_WIDTH, NA_WIDTH,
            DIFF_QKV_WIDTH, DIFF_WIDTH,
            N_BRANCH * D_MODEL)
D_IN = sum(IN_SIZES)

kernel_name = "hybrid_rwkv7_natten_diffattn_parallel_block"


def _split(z, sizes):
    return jnp.split(z, [int(i) for i in np.cumsum(sizes)[:-1]], axis=-1)


def _rmsnorm(x, g, eps=RMS_EPS):
    xf = x.astype(jnp.float32)
    y = xf * lax.rsqrt(jnp.mean(xf * xf, axis=-1, keepdims=True) + eps)
    return (y * g.astype(jnp.float32)).astype(x.dtype)


def _token_shift(u, mu_prev, mu_next):
    zeros = jnp.zeros_like(u[:, :1])
    u_prev = jnp.concatenate([zeros, u[:, :-1]], axis=1)
    u_next = jnp.concatenate([u[:, 1:], zeros], axis=1)
    return u + mu_prev * (u_prev - u) + mu_next * (u_next - u)


def _rwkv_scan(r, decay, k, v, kk, a, s0, reverse, with_out):
    seq = lambda t: jnp.moveaxis(t, 1, 0)
    xs = (seq(decay), seq(k), seq(v), seq(-kk), seq(kk * a), seq(r) if with_out else None)

    def step(s, inp):
        w_t, k_t, v_t, a_t, b_t, r_t = inp
        sa = jnp.einsum('bhij,bhj->bhi', s, a_t)
        s = s * w_t[:, :, None, :] + sa[..., None] * b_t[:, :, None, :] + v_t[..., None] * k_t[:, :, None, :]
        y = jnp.einsum('bhij,bhj->bhi', s, r_t) if with_out else None
        return s, y

    s_fin, ys = lax.scan(step, s0, xs, reverse=reverse)
    return s_fin, (jnp.moveaxis(ys, 0, 1) if with_out else None)


def _rwkv_branch(u_x, u_c, k_k, k_a, r_k, w0, w_up, a0, a_up, ln_g, ln_b, with_ctx_out):
    f32 = jnp.float32

    def prep(u):
        B, T, _ = u.shape
        r, k, v, wf, wb, af, ab = _split(u.astype(f32), (RWKV_WIDTH, RWKV_WIDTH, RWKV_WIDTH,
                                                         DECAY_LORA, DECAY_LORA, AAA_LORA, AAA_LORA))
        hd = lambda t: t.reshape(B, T, RWKV_HEADS, RWKV_HEAD_DIM)
        kk = hd(k * k_k)
        kk = kk / jnp.maximum(jnp.sqrt(jnp.sum(kk * kk, axis=-1, keepdims=True)), 1e-12)
        dirs = []
        for d, (wd, ad) in enumerate(((wf, af), (wb, ab))):
            w_log = -jax.nn.softplus(-(w0[d] + jnp.tanh(wd) @ w_up[d])) - 0.5
            a = jax.nn.sigmoid(a0[d] + ad @ a_up[d])
            k_dir = k * (1.0 + (a - 1.0) * k_a)
            dirs.append((hd(jnp.exp(-jnp.exp(w_log))), hd(k_dir), hd(a)))
        return hd(r), hd(v), kk, dirs

    def readout(y, r, v, k_dirs):
        B, T = y.shape[:2]
        mu = jnp.mean(y, axis=-1, keepdims=True)
        var = jnp.mean(jnp.square(y - mu), axis=-1, keepdims=True)
        yn = ((y - mu) * lax.rsqrt(var + GN_EPS)).reshape(B, T, RWKV_WIDTH) * ln_g + ln_b
        bonus = sum(jnp.sum(r * kd * r_k, axis=-1, keepdims=True) * v for kd in k_dirs)
        return yn + bonus.reshape(B, T, RWKV_WIDTH)

    r_c, v_c, kk_c, dirs_c = prep(u_c)
    r_x, v_x, kk_x, dirs_x = prep(u_x)
    s0 = jnp.zeros((u_x.shape[0], RWKV_HEADS, RWKV_HEAD_DIM, RWKV_HEAD_DIM), f32)
    ys_x, ys_c = [], []
    for d, reverse in enumerate((False, True)):
        dec_c, k_c, a_c = dirs_c[d]
        s_ctx, y_cd = _rwkv_scan(r_c, dec_c, k_c, v_c, kk_c, a_c, s0, reverse, with_ctx_out)
        dec_x, k_x, a_x = dirs_x[d]
        _, y_xd = _rwkv_scan(r_x, dec_x, k_x, v_x, kk_x, a_x, s_ctx, reverse, True)
        ys_x.append(y_xd)
        ys_c.append(y_cd)
    o_x = readout(ys_x[0] + ys_x[1], r_x, v_x, [dirs_x[0][1], dirs_x[1][1]])
    o_c = readout(ys_c[0] + ys_c[1], r_c, v_c, [dirs_c[0][1], dirs_c[1][1]]) if with_ctx_out else None
    return o_x, o_c


def _na_branch(qkv_x, qkv_c, rpb, with_ctx_out):
    f32 = jnp.float32
    B, S, _ = qkv_x.shape
    rows = S // GRID_W
    wr = min(NA_WIN_R, rows)
    scale = NA_HEAD_DIM ** -0.5

    def heads(t):
        return t.reshape(t.shape[0], t.shape[1], NA_HEADS, NA_HEAD_DIM).transpose(0, 2, 1, 3)

    q, k, v = (heads(t) for t in jnp.split(qkv_x, 3, axis=-1))
    qc, kc, vc = (heads(t) for t in jnp.split(qkv_c, 3, axis=-1))
    grid = lambda t: t.reshape(B, NA_HEADS, rows, GRID_W, NA_HEAD_DIM)
    q, k, v = grid(q) * scale, grid(k), grid(v)

    r_idx = jnp.arange(rows)
    r_start = jnp.clip(r_idx - wr // 2, 0, rows - wr)
    key_rows = r_start[:, None] + jnp.arange(wr)[None, :]
    kg = k[:, :, key_rows]
    vg = v[:, :, key_rows]
    c_idx = jnp.arange(GRID_W)
    c_start = jnp.clip(c_idx - NA_WIN_C // 2, 0, GRID_W - NA_WIN_C)
    col_ok = (c_idx[None, :] >= c_start[:, None]) & (c_idx[None, :] < c_start[:, None] + NA_WIN_C)
    dr = key_rows - r_idx[:, None] + (NA_WIN_R - 1)
    dc = jnp.clip(c_idx[None, :] - c_idx[:, None], -(NA_WIN_C - 1), NA_WIN_C - 1) + (NA_WIN_C - 1)
    bias = rpb[:, dr[:, None, :, None], dc[None, :, None, :]].astype(f32)

    s_nb = jnp.einsum('bhrqd,bhrwkd->bhrqwk', q, kg).astype(f32) + bias
    s_nb = jnp.where(col_ok[:, None, :], s_nb, NEG_INF)
    s_c = jnp.einsum('bhrqd,bhld->bhrql', q, kc).astype(f32)
    m = jnp.maximum(jnp.max(s_nb, axis=(-2, -1)), jnp.max(s_c, axis=-1))[..., None]
    p_nb = jnp.exp(s_nb - m[..., None])
    p_c = jnp.exp(s_c - m)
    denom = jnp.sum(p_nb, axis=(-2, -1)) + jnp.sum(p_c, axis=-1)
    o = (jnp.einsum('bhrqwk,bhrwkd->bhrqd', p_nb, vg.astype(f32))
         + jnp.einsum('bhrql,bhld->bhrqd', p_c, vc.astype(f32))) / denom[..., None]
    o_x = o.reshape(B, NA_HEADS, S, NA_HEAD_DIM).transpose(0, 2, 1, 3).reshape(B, S, NA_WIDTH)
    o_c = None
    if with_ctx_out:
        pc = jax.nn.softmax(jnp.einsum('bhqd,bhkd->bhqk', qc * scale, kc).astype(f32), axis=-1)
        oc = jnp.einsum('bhqk,bhkd->bhqd', pc, vc.astype(f32))
        o_c = oc.transpose(0, 2, 1, 3).reshape(B, qkv_c.shape[1], NA_WIDTH)
    return o_x, o_c


def _rope_tables(n_tokens):
    t = jnp.arange(n_tokens, dtype=jnp.int32)
    row = (t // GRID_W).astype(jnp.float32)
    col = (t % GRID_W).astype(jnp.float32)
    axis_dim = DIFF_QK_DIM // 2
    inv = ROPE_THETA ** (-jnp.arange(0, axis_dim, 2, dtype=jnp.float32) / axis_dim)
    ar = row[:, None] * inv
    ac = col[:, None] * inv
    ang = jnp.concatenate([ar, ar, ac, ac], axis=-1)
    return jnp.cos(ang), jnp.sin(ang)


def _apply_rope_2d(x, cos, sin):
    xf = x.astype(jnp.float32)
    xs = xf.reshape(xf.shape[:-1] + (2, 2, DIFF_QK_DIM // 4))
    rot = jnp.stack([-xs[..., 1, :], xs[..., 0, :]], axis=-2).reshape(xf.shape)
    cb = cos[None, :, None, None, :]
    sb = sin[None, :, None, None, :]
    return (xf * cb + rot * sb).astype(x.dtype)


def _diff_branch(z_x, z_c, lam_q, lam_k, subln_g, lambda_init, cos, sin, with_ctx_out):
    f32 = jnp.float32
    qk_w = DIFF_HEADS * 2 * DIFF_QK_DIM
    scale = DIFF_QK_DIM ** -0.5

    def split(z):
        B, T, _ = z.shape
        q, k, v = _split(z, (qk_w, qk_w, DIFF_WIDTH))
        q = q.reshape(B, T, DIFF_HEADS, 2, DIFF_QK_DIM)
        k = k.reshape(B, T, DIFF_HEADS, 2, DIFF_QK_DIM)
        v = v.reshape(B, T, DIFF_HEADS, DIFF_V_DIM)
        return q, k, v

    qx, kx, vx = split(z_x)
    qc, kc, vc = split(z_c)
    qx = _apply_rope_2d(qx, cos, sin)
    kx = _apply_rope_2d(kx, cos, sin)
    to_h = lambda t: t.transpose(0, 2, 3, 1, 4)
    qx, kx, qc, kc = to_h(qx), to_h(kx), to_h(qc), to_h(kc)
    vx, vc = vx.transpose(0, 2, 1, 3), vc.transpose(0, 2, 1, 3)
    lam = (jnp.exp(jnp.sum(lam_q[0] * lam_k[0]).astype(f32))
           - jnp.exp(jnp.sum(lam_q[1] * lam_k[1]).astype(f32)) + lambda_init)

    def attend(qb, keys, vals):
        s = jnp.einsum('bhmqd,bhmkd->bhmqk', qb, keys).astype(f32) * scale
        p = jax.nn.softmax(s, axis=-1)
        w = p[:, :, 0] - lam * p[:, :, 1]
        return jnp.einsum('bhqk,bhkv->bhqv', w, vals.astype(f32))

    def post(o):
        o = o.transpose(0, 2, 1, 3)
        o = _rmsnorm(o, subln_g, SUBLN_EPS) * (1.0 - lambda_init)
        return o.reshape(o.shape[0], o.shape[1], DIFF_WIDTH)

    B, H, _, S, d = qx.shape
    keys = jnp.concatenate([kx, kc], axis=3)
    vals = jnp.concatenate([vx, vc], axis=2)
    nblk = S // Q_BLOCK
    qb = jnp.moveaxis(qx.reshape(B, H, 2, nblk, Q_BLOCK, d), 3, 0)
    ob = lax.map(lambda blk: attend(blk, keys, vals), qb)
    o_x = post(jnp.moveaxis(ob, 0, 2).reshape(B, H, S, DIFF_V_DIM))
    o_c = post(attend(qc, kc, vc)) if with_ctx_out else None
    return o_x, o_c


def _mixer(hx, hc, w_in, mu, k_k, k_a, r_k, w0, w_up, a0, a_up, ln_g, ln_b, rpb,
           lam_q, lam_k, subln_g, w_branch, w_out, lambda_init, cos, sin, with_ctx_out):
    zx = hx @ w_in
    zc = hc @ w_in
    rw_x, rg_x, na_x, ng_x, df_x, dg_x, mg_x = _split(zx, IN_SIZES)
    rw_c, rg_c, na_c, ng_c, df_c, dg_c, mg_c = _split(zc, IN_SIZES)
    rw_x = _token_shift(rw_x, mu[0], mu[1])
    rw_c = _token_shift(rw_c, mu[0], mu[1])
    o_rw_x, o_rw_c = _rwkv_branch(rw_x, rw_c, k_k, k_a, r_k, w0, w_up, a0, a_up, ln_g, ln_b, with_ctx_out)
    o_na_x, o_na_c = _na_branch(na_x, na_c, rpb, with_ctx_out)
    o_df_x, o_df_c = _diff_branch(df_x, df_c, lam_q, lam_k, subln_g, lambda_init, cos, sin, with_ctx_out)

    def merge(outs, gates, mg):
        o = jnp.stack([ob.astype(g.dtype) * jax.nn.silu(g) for ob, g in zip(outs, gates)], axis=2)
        yb = jnp.einsum('btnc,ncd->btnd', o, w_branch)
        gb = jax.nn.sigmoid(mg.reshape(mg.shape[:2] + (N_BRANCH, D_MODEL)))
        return jnp.einsum('btd,de->bte', jnp.sum(gb * yb, axis=2), w_out).astype(hx.dtype)

    y_x = merge((o_rw_x, o_na_x, o_df_x), (rg_x, ng_x, dg_x), mg_x)
    y_c = merge((o_rw_c, o_na_c, o_df_c), (rg_c, ng_c, dg_c), mg_c) if with_ctx_out else None
    return y_x, y_c


def setup_inputs(seed: int = 0) -> dict:
    key = jax.random.key(seed)
    ks = jax.random.split(key, 26)
    f32 = jnp.float32
    D, L = D_MODEL, DEPTH
    nrm = lambda k, shape, s: jax.random.normal(k, shape, f32) * s
    return {
        "x": nrm(ks[0], (BATCH, SEQ, D), 1.0),
        "c": nrm(ks[1], (BATCH, D), 1.0),
        "ctx": nrm(ks[2], (BATCH, CTX_LEN, D), 1.0),
        "c_ctx": nrm(ks[3], (D,), 1.0),
        "w_mod": nrm(ks[4], (L, D, 3 * D), 0.5 * D ** -0.5),
        "b_mod": nrm(ks[5], (L, 3 * D), 0.02),
        "g_pre": 1.0 + nrm(ks[6], (L, D), 0.05),
        "g_post": 1.0 + nrm(ks[7], (L, D), 0.05),
        "w_in": nrm(ks[8], (L, D, D_IN), D ** -0.5),
        "shift_mu": jax.random.uniform(ks[9], (L, 2, RWKV_SHIFT_WIDTH), f32, 0.0, 0.5),
        "k_k": 0.85 + nrm(ks[10], (L, RWKV_WIDTH), 0.05),
        "k_a": 1.0 + nrm(ks[11], (L, RWKV_WIDTH), 0.05),
        "r_k": nrm(ks[12], (L, RWKV_HEADS, RWKV_HEAD_DIM), 0.1),
        "w0": jax.random.uniform(ks[13], (L, 2, RWKV_WIDTH), f32, -6.0, -1.0),
        "w_up": nrm(ks[14], (L, 2, DECAY_LORA, RWKV_WIDTH), 0.1),
        "a0": nrm(ks[15], (L, 2, RWKV_WIDTH), 0.1),
        "a_up": nrm(ks[16], (L, 2, AAA_LORA, RWKV_WIDTH), 0.5 * AAA_LORA ** -0.5),
        "ln_x_g": 1.0 + nrm(ks[17], (L, RWKV_WIDTH), 0.05),
        "ln_x_b": nrm(ks[18], (L, RWKV_WIDTH), 0.02),
        "rpb": nrm(ks[19], (L, NA_HEADS, 2 * NA_WIN_R - 1, 2 * NA_WIN_C - 1), 0.1),
        "lam_q": nrm(ks[20], (L, 2, DIFF_QK_DIM), 0.1),
        "lam_k": nrm(ks[21], (L, 2, DIFF_QK_DIM), 0.1),
        "diff_subln": 1.0 + nrm(ks[22], (L, DIFF_V_DIM), 0.05),
        "w_branch": nrm(ks[23], (L, N_BRANCH, BRANCH_WIDTH, D), BRANCH_WIDTH ** -0.5),
        "w_out": nrm(ks[24], (L, D, D), D ** -0.5),
    }


def reference(x, c, ctx, c_ctx, w_mod, b_mod, g_pre, g_post, w_in, shift_mu, k_k, k_a, r_k,
              w0, w_up, a0, a_up, ln_x_g, ln_x_b, rpb, lam_q, lam_k, diff_subln, w_branch, w_out):
    S = x.shape[1]
    cos, sin = _rope_tables(S)
    hc = ctx
    for l in range(DEPTH):
        last = l == DEPTH - 1
        lambda_init = 0.8 - 0.6 * math.exp(-0.3 * l)
        mod_x = jax.nn.silu(c) @ w_mod[l] + b_mod[l]
        mod_c = jax.nn.silu(c_ctx) @ w_mod[l] + b_mod[l]
        sh_x, sc_x, gt_x = jnp.split(mod_x[:, None, :], 3, axis=-1)
        sh_c, sc_c, gt_c = jnp.split(mod_c, 3, axis=-1)
        hx = _rmsnorm(x, g_pre[l]) * (1.0 + sc_x) + sh_x
        hcn = _rmsnorm(hc, g_pre[l]) * (1.0 + sc_c) + sh_c
        y_x, y_c = _mixer(hx, hcn, w_in[l], shift_mu[l], k_k[l], k_a[l], r_k[l], w0[l], w_up[l],
                          a0[l], a_up[l], ln_x_g[l], ln_x_b[l], rpb[l], lam_q[l], lam_k[l],
                          diff_subln[l], w_branch[l], w_out[l], lambda_init, cos, sin,
                          not last)
        x = (x + gt_x * _rmsnorm(y_x, g_post[l])).astype(x.dtype)
        if not last:
            hc = (hc + gt_c * _rmsnorm(y_c, g_post[l])).astype(hc.dtype)
    return x
```

```python
import numpy as np
import ml_dtypes
import concourse.bass as bass
import concourse.mybir as mybir
from concourse.bass_utils import run_bass_kernel_spmd

F32, BF16 = mybir.dt.float32, mybir.dt.bfloat16
AF = mybir.ActivationFunctionType
ALU = mybir.AluOpType
AX = mybir.AxisListType

D = 1024
S = 4096
CT = 256
T = S + CT
NT = T // 128
DIN = 9472
L = 4
NH = 8
ARENA_BASE = 18432
ARENA_END = 229376


class Rec:
    __slots__ = ("eng", "fn", "deps", "inc", "cnt", "dma", "sem", "val")

    def __init__(self, eng, fn, deps, dma=False, sem=None, val=0):
        self.eng = eng
        self.fn = fn
        self.deps = deps
        self.inc = False
        self.cnt = 0
        self.dma = dma
        self.sem = sem
        self.val = val


class Tl:
    def __init__(self, t, name):
        self.t = t
        self.name = name

    def __getitem__(self, k):
        return self.t[k]


class Em:
    CENG = ("pe", "act", "dve", "pool")
    ENG = ("pe", "act", "dve", "pool", "sp")
    NSLOT = 16

    def __init__(self, nc, same_sync=True):
        self.nc = nc
        self.same_sync = same_sync
        self.prog = {e: [] for e in self.ENG}
        self.lastw = {}
        self.readers = {}
        self.sems = {e: nc.alloc_semaphore("sem_" + e) for e in self.CENG}
        self.dq = {}
        for q in ("sp", "pool", "act"):
            self.dq[q] = dict(
                slots=[nc.alloc_semaphore("dq_%s_%d" % (q, i)) for i in range(self.NSLOT)],
                last=[None] * self.NSLOT, cnt=[0] * self.NSLOT, next=0)
        self.off = ARENA_BASE
        self.ntile = 0
        self.lastc = {e: None for e in self.CENG}

    def tile(self, shape, dtype, name="t"):
        esz = 4 if dtype == F32 else 2
        n = 1
        for s in shape[1:]:
            n *= s
        nbytes = (n * esz + 63) // 64 * 64
        assert self.off + nbytes <= ARENA_END, "SBUF arena overflow %s %d" % (name, self.off + nbytes)
        self.ntile += 1
        nm = "%s_%d" % (name, self.ntile)
        t = self.nc.alloc_sbuf_tensor_at(nm, list(shape), dtype, offset=self.off)
        self.off += nbytes
        return Tl(t, nm)

    def mark(self):
        return self.off

    def release(self, mark):
        self.off = mark

    def _deps(self, rd, wr):
        deps = []
        for r in rd:
            w = self.lastw.get(r)
            if w is not None:
                deps.append((w, 1))
        for w in wr:
            lw = self.lastw.get(w)
            if lw is not None:
                deps.append((lw, 0))
            deps.extend((x, 0) for x in self.readers.get(w, ()))
        return deps

    def _commit(self, rec, rd, wr):
        for w in wr:
            self.lastw[w] = rec
            self.readers[w] = []
        for r in rd:
            if r in wr:
                continue
            lst = self.readers.setdefault(r, [])
            if rec.dma:
                lst[:] = [x for x in lst if not (x.dma and x.sem is rec.sem)]
            else:
                lst[:] = [x for x in lst if x.dma or x.eng != rec.eng]
            lst.append(rec)

    def op(self, eng, fn, rd=(), wr=()):
        rec = Rec(eng, fn, self._deps(rd, wr))
        self.prog[eng].append(rec)
        self._commit(rec, rd, wr)
        self.lastc[eng] = rec
        return rec

    def dma(self, q, out, in_, rd=(), wr=(), **kw):
        dq = self.dq[q]
        i = dq["next"]
        dq["next"] = (i + 1) % self.NSLOT
        deps = self._deps(rd, wr)
        if dq["last"][i] is not None:
            deps.append((dq["last"][i], 1))
        dq["cnt"][i] += 16
        rec = Rec(q, (lambda e: e.dma_start(out=out, in_=in_, **kw)), deps, dma=True,
                  sem=dq["slots"][i], val=dq["cnt"][i])
        dq["last"][i] = rec
        self.prog[q].append(rec)
        self._commit(rec, rd, wr)
        return rec

    def barrier(self):
        lasts = [r for r in self.lastc.values() if r is not None]
        for q in self.dq.values():
            lasts.extend(r for r in q["last"] if r is not None)
        for e in self.ENG:
            self.prog[e].append(Rec(e, None, [(x, 1) for x in lasts]))
        self.lastw.clear()
        self.readers.clear()

    def finalize(self):
        for e in self.ENG:
            for rec in self.prog[e]:
                for d, raw in rec.deps:
                    if d.dma:
                        continue
                    if d.eng == rec.eng and not rec.dma and (e == "pe" or not raw or not self.same_sync):
                        continue
                    d.inc = True
        for e in self.CENG:
            c = 0
            for rec in self.prog[e]:
                if (not rec.dma) and rec.inc:
                    c += 1
                    rec.cnt = c
        em = self

        def replay(e, eng):
            waited = {}
            for rec in em.prog[e]:
                for d, raw in rec.deps:
                    if d.dma:
                        key, sem, val = id(d.sem), d.sem, d.val
                    else:
                        if d.eng == rec.eng and not rec.dma and (e == "pe" or not raw or not em.same_sync):
                            continue
                        key, sem, val = d.eng, em.sems[d.eng], d.cnt
                    if waited.get(key, 0) < val:
                        eng.wait_ge(sem, val)
                        waited[key] = val
                if rec.fn is None:
                    continue
                inst = rec.fn(eng)
                if rec.dma:
                    inst.then_inc(rec.sem, 16)
                elif rec.inc:
                    inst.then_inc(em.sems[e], 1)

        with self.nc.Block() as block:
            @block.tensor
            def _(eng):
                replay("pe", eng)

            @block.scalar
            def _(eng):
                replay("act", eng)

            @block.vector
            def _(eng):
                replay("dve", eng)

            @block.gpsimd
            def _(eng):
                replay("pool", eng)

            @block.sync
            def _(eng):
                replay("sp", eng)


class Prog:
    def __init__(self, stages=("all",), debug=(), nlayers=L, feed=()):
        self.stages = stages
        self.debug = set(debug)
        self.feed = set(feed)
        self.nlayers = nlayers
        self.nc = nc = bass.Bass("TRN2", target_bir_lowering=False)
        self.em = Em(nc)
        self.din = {}
        self.dscr = {}
        self.ps = [Tl(nc.alloc_psum_tensor("psf%d" % i, [128, 512], F32), "psf%d" % i) for i in range(7)]
        self.psb = Tl(nc.alloc_psum_tensor("psb", [128, 1024], BF16), "psb")

    def inp(self, name, shape, dtype=F32):
        t = self.nc.dram_tensor(name, list(shape), dtype, kind="ExternalInput")
        self.din[name] = t
        return t.ap()

    def scr(self, name, shape, dtype=F32):
        kind = "ExternalOutput" if name in self.debug else ("ExternalInput" if name in self.feed else "Internal")
        t = self.nc.dram_tensor(name, list(shape), dtype, kind=kind)
        self.dscr[name] = t
        if name in self.feed:
            self.din[name] = t
        return t.ap()

    def declare(self):
        I, Sc = self.inp, self.scr
        self.xin = I("xin", [S, D])
        self.cin = I("cin", [CT, D])
        self.cv = I("cv", [128, 8, 2])
        self.w_mod = I("w_mod", [L, D, 3 * D])
        self.b_mod = I("b_mod", [L, 3 * D])
        self.g_pre_fm = I("g_pre_fm", [L, 128, 8])
        self.g_post = I("g_post", [L, D])
        self.w_in = I("w_in", [L, D, DIN])
        self.ident_d = I("ident", [128, 128])
        self.rope_rm_d = I("rope_rm", [128, 128])
        self.cosT_d = I("cosT", [128, S])
        self.sinT_d = I("sinT", [128, S])
        self.out = self.nc.dram_tensor("out", [S, D], F32, kind="ExternalOutput").ap()
        self.xres = Sc("xres", [T, D])
        self.mod_fm = Sc("mod_fm", [L, 128, 32])
        self.modG = Sc("modG", [L, 2, 128, D])
        self.z_rw = Sc("z_rw", [T, 1792])
        self.zg = Sc("zg", [T, 1536], BF16)
        self.zmg = Sc("zmg", [T, 3072], BF16)
        self.zT_naq = Sc("zT_naq", [512, T], BF16)
        self.zT_nak = Sc("zT_nak", [512, T], BF16)
        self.z_nav = Sc("z_nav", [T, 8 * 65], BF16)
        self.zT_dfq = Sc("zT_dfq", [512, T], BF16)
        self.zT_dfk = Sc("zT_dfk", [512, T], BF16)
        self.z_dfv = Sc("z_dfv", [T, 4 * 129], BF16)
        self.mu = I("mu", [L, 2, 1792])
        self.pvec = I("pvec", [L, 5, 512])
        self.w0 = I("w0", [L, 2, 512])
        self.a0 = I("a0", [L, 2, 512])
        self.w_up = I("w_up", [L, 2, 64, 512])
        self.a_up = I("a_up", [L, 2, 64, 512])
        self.lamq = I("lamq", [L, 2, 64])
        self.lamk = I("lamk", [L, 2, 64])
        self.subln = I("subln", [L, 128])
        self.nabias = I("nabias", [L, 8, 64, 15, 64])
        self.w_branch = I("w_branch", [L, 3, 512, D])
        self.w_out = I("w_out", [L, D, D])
        self.tri_d = I("tri", [2, 64, 64])
        self.mklt_d = I("mklt", [2, 64, 128])
        self.mkl_d = I("mkl", [2, 64, 64])
        self.rwp = Sc("rwp", [9, T, 512])
        self.rwbs = Sc("rwbs", [T, 8])
        self.ysc = Sc("ysc", [2, T, 512])
        self.og = Sc("og", [T, 1536], BF16)

    def res_src(self, l, tt):
        if l == 0:
            return (self.cin[tt * 128:(tt + 1) * 128, :] if tt < 2
                    else self.xin[(tt - 2) * 128:(tt - 1) * 128, :])
        return self.xres[tt * 128:(tt + 1) * 128, :]

    def phase0(self):
        em = self.em
        m0 = em.mark()
        ident = em.tile([128, 128], F32, "ident")
        em.dma("sp", ident[:, :], self.ident_d[:, :], wr=[ident])
        cv = em.tile([128, 8, 2], F32, "cv")
        sg = em.tile([128, 8, 2], F32, "sg")
        sv = em.tile([128, 8, 2], F32, "sv")
        sb = em.tile([128, 2, 8, 128], F32, "sb")
        em.dma("sp", cv[:, :, :], self.cv[:, :, :], wr=[cv])
        em.op("act", lambda e: e.activation(out=sg[:, :, :], in_=cv[:, :, :], func=AF.Sigmoid), rd=[cv], wr=[sg])
        em.op("dve", lambda e: e.tensor_tensor(out=sv[:, :, :], in0=cv[:, :, :], in1=sg[:, :, :], op=ALU.mult),
              rd=[cv, sg], wr=[sv])
        for v in range(2):
            for k in range(8):
                em.op("dve", lambda e, v=v, k=k: e.tensor_copy(
                    out=sb[:, v, k, :], in_=sv[:, k, v:v + 1].to_broadcast([128, 128])), rd=[sv], wr=[sb])
        wbuf = [em.tile([128, 8, 512], F32, "wmod") for _ in range(2)]
        modb = em.tile([128, 2, 3 * D], F32, "modb")
        bmod = em.tile([128, 3 * D], F32, "bmod")
        gpost = em.tile([128, D], F32, "gpost")
        gpre = em.tile([128, 8], F32, "gpre")
        tmp = em.tile([128, 128], F32, "tmp")
        fm = em.tile([128, 4, 8], F32, "fm")
        Gt = em.tile([128, 2, D], F32, "Gt")
        it = 0
        for l in range(self.nlayers):
            em.dma("sp", bmod[:, :], self.b_mod[l].partition_broadcast(128), wr=[bmod])
            em.dma("sp", gpost[:, :], self.g_post[l].partition_broadcast(128), wr=[gpost])
            em.dma("sp", gpre[:, :], self.g_pre_fm[l], wr=[gpre])
            for g in range(6):
                w = wbuf[it % 2]
                it += 1
                em.dma("sp", w[:, :, :],
                       self.w_mod[l, :, g * 512:(g + 1) * 512].rearrange("(k p) c -> p k c", p=128), wr=[w])
                for v in range(2):
                    ps = self.ps[v]
                    for k in range(8):
                        em.op("pe", lambda e, v=v, k=k, w=w, ps=ps: e.matmul(
                            ps[:, :], lhsT=sb[:, v, k, :], rhs=w[:, k, :], start=(k == 0), stop=(k == 7)),
                            rd=[sb, w], wr=[ps])
                    em.op("dve", lambda e, v=v, g=g, ps=ps: e.tensor_tensor(
                        out=modb[:, v, g * 512:(g + 1) * 512], in0=ps[:, :], in1=bmod[:, g * 512:(g + 1) * 512],
                        op=ALU.add), rd=[ps, bmod], wr=[modb])
            for v in range(2):
                for which in range(2):
                    for k in range(8):
                        c0 = which * D + k * 128
                        em.op("dve", lambda e, v=v, c0=c0: e.tensor_tensor(
                            out=tmp[:, :], in0=modb[:, v, c0:c0 + 128], in1=ident[:, :], op=ALU.mult),
                            rd=[modb, ident], wr=[tmp])
                        slot = 2 * v + (1 - which)
                        em.op("dve", lambda e, slot=slot, k=k: e.reduce_sum(
                            out=fm[:, slot, k:k + 1], in_=tmp[:, :], axis=AX.X), rd=[tmp], wr=[fm])
                em.op("dve", lambda e, v=v: e.scalar_tensor_tensor(
                    out=fm[:, 2 * v, :], in0=fm[:, 2 * v, :], scalar=1.0, in1=gpre[:, :],
                    op0=ALU.add, op1=ALU.mult), rd=[fm, gpre], wr=[fm])
                em.op("dve", lambda e, v=v: e.tensor_tensor(
                    out=Gt[:, v, :], in0=modb[:, v, 2 * D:3 * D], in1=gpost[:, :], op=ALU.mult),
                    rd=[modb, gpost], wr=[Gt])
            em.dma("sp", self.mod_fm[l], fm[:, :, :].rearrange("p a k -> p (a k)"), rd=[fm], wr=[("mod_fm", l)])
            em.dma("sp", self.modG[l].rearrange("v p d -> p v d"), Gt[:, :, :], rd=[Gt], wr=[("modG", l)])
        em.barrier()
        em.release(m0)

    GROUPS = [
        (0, 512, "rw", 0), (512, 512, "rw", 512), (1024, 512, "rw", 1024), (1536, 256, "rw", 1536),
        (1792, 512, "gate", 0),
        (2304, 512, "fm", "naq"), (2816, 512, "fm", "nak"), (3328, 512, "nav", 0),
        (3840, 512, "gate", 512),
        (4352, 512, "fm", "dfq"), (4864, 512, "fm", "dfk"), (5376, 512, "dfv", 0),
        (5888, 512, "gate", 1024),
        (6400, 512, "mg", 0), (6912, 512, "mg", 512), (7424, 512, "mg", 1024),
        (7936, 512, "mg", 1536), (8448, 512, "mg", 2048), (8960, 512, "mg", 2560),
    ]

    def phaseA(self, l):
        em = self.em
        m0 = em.mark()
        ident = em.tile([128, 128], F32, "ident")
        em.dma("sp", ident[:, :], self.ident_d[:, :], wr=[ident])
        fm = em.tile([128, 4, 8], F32, "fm")
        em.dma("sp", fm[:, :, :].rearrange("p a k -> p (a k)"), self.mod_fm[l], rd=[("mod_fm", l)], wr=[fm])
        hT = em.tile([128, 8, T], BF16, "hT")
        xb = [em.tile([128, D], F32, "xb") for _ in range(3)]
        xn = [em.tile([128, D], F32, "xn") for _ in range(2)]
        junk = em.tile([128, D], F32, "junk")
        st = [em.tile([128, 4], F32, "st") for _ in range(2)]
        for tt in range(NT):
            x_t, xn_t, s_t = xb[tt % 3], xn[tt % 2], st[tt % 2]
            v = 1 if tt < 2 else 0
            em.dma("sp", x_t[:, :], self.res_src(l, tt), rd=([("xres", tt)] if l > 0 else []), wr=[x_t])
            em.op("pool", lambda e, s_t=s_t: e.memset(s_t[:, :], 0.0), wr=[s_t])
            em.op("act", lambda e, x_t=x_t, s_t=s_t: e.activation(
                out=junk[:, :], in_=x_t[:, :], func=AF.Square, accum_out=s_t[:, 0:1]), rd=[x_t, s_t], wr=[junk, s_t])
            em.op("act", lambda e, s_t=s_t: e.activation(
                out=s_t[:, 1:2], in_=s_t[:, 0:1], func=AF.Sqrt, bias=1e-6, scale=1.0 / D), rd=[s_t], wr=[s_t])
            em.op("dve", lambda e, s_t=s_t: e.reciprocal(out=s_t[:, 2:3], in_=s_t[:, 1:2]), rd=[s_t], wr=[s_t])
            em.op("act", lambda e, x_t=x_t, xn_t=xn_t, s_t=s_t: e.activation(
                out=xn_t[:, :], in_=x_t[:, :], func=AF.Copy, scale=s_t[:, 2:3]), rd=[x_t, s_t], wr=[xn_t])
            for half in range(2):
                ps = self.ps[2 * (tt % 2) + half]
                for j in range(4):
                    k = half * 4 + j
                    em.op("pe", lambda e, ps=ps, j=j, k=k, xn_t=xn_t: e.transpose(
                        out=ps[:, j * 128:(j + 1) * 128], in_=xn_t[:, k * 128:(k + 1) * 128], identity=ident[:, :]),
                        rd=[xn_t, ident], wr=[ps])
                for j in range(4):
                    k = half * 4 + j
                    dst = hT[:, k, tt * 128:(tt + 1) * 128]
                    if j % 2 == 0:
                        em.op("dve", lambda e, ps=ps, j=j, k=k, v=v, dst=dst: e.tensor_scalar(
                            out=dst, in0=ps[:, j * 128:(j + 1) * 128], scalar1=fm[:, 2 * v, k:k + 1],
                            scalar2=fm[:, 2 * v + 1, k:k + 1], op0=ALU.mult, op1=ALU.add),
                            rd=[ps, fm], wr=[("hT", tt)])
                    else:
                        em.op("act", lambda e, ps=ps, j=j, k=k, v=v, dst=dst: e.activation(
                            out=dst, in_=ps[:, j * 128:(j + 1) * 128], func=AF.Identity,
                            bias=fm[:, 2 * v + 1, k:k + 1], scale=fm[:, 2 * v, k:k + 1]),
                            rd=[ps, fm], wr=[("hT", tt)])
        cosT = em.tile([128, S], F32, "cosT")
        sinT = em.tile([128, S], F32, "sinT")
        wp = em.tile([128, 8, 512], BF16, "wperm")
        em.dma("sp", cosT[:, :], self.cosT_d[:, :], wr=[cosT])
        em.dma("sp", sinT[:, :], self.sinT_d[:, :], wr=[sinT])
        wb = [em.tile([128, 8, 512], BF16, "win") for _ in range(3)]
        stF = [em.tile([128, 512], F32, "stF") for _ in range(4)]
        stB = [em.tile([128, 512], BF16, "stB") for _ in range(4)]
        stV = [em.tile([128, 8, 65], BF16, "stV") for _ in range(2)]
        stDV = [em.tile([128, 4, 129], BF16, "stDV") for _ in range(2)]
        for t_ in stV + stDV:
            em.op("pool", lambda e, t_=t_: e.memset(t_[:, :, :], 1.0), wr=[t_])
        cnt = dict(ps=0, F=0, B=0, V=0, DV=0, alt=0)

        def nxt(key, pool):
            i = cnt[key]
            cnt[key] += 1
            return pool[i % len(pool)]

        def alt():
            cnt["alt"] += 1
            return "act" if cnt["alt"] % 2 else "dve"

        def copy_op(eng, out, in_, rd, wr):
            if eng == "act":
                em.op("act", lambda e: e.activation(out=out, in_=in_, func=AF.Copy), rd=rd, wr=wr)
            else:
                em.op(eng, lambda e: e.tensor_copy(out=out, in_=in_), rd=rd, wr=wr)

        blocks = [(0, 256)] + [(256 + 512 * i, 512) for i in range(8)]
        zT = dict(naq=self.zT_naq, nak=self.zT_nak, dfq=self.zT_dfq, dfk=self.zT_dfk)
        for gi, (c0, n, kind, dof) in enumerate(self.GROUPS):
            w = wb[gi % 3]
            em.dma("pool", w[:, :, 0:n], self.w_in[l, :, c0:c0 + n].rearrange("(k p) c -> p k c", p=128), wr=[w])
            if kind == "fm":
                dst = zT[dof]
                import os
                RV = os.environ.get("ROPE", "1")
                rope = dof in ("dfq", "dfk") and RV != "0"
                if rope:
                    wv = w[:, :, :].rearrange("p k (g h i) -> p k g h i", g=16, h=2, i=16)
                    wpv = wp[:, :, :].rearrange("p k (g h i) -> p k g h i", g=16, h=2, i=16)
                    for hh in range(2):
                        em.op("pool", lambda e, hh=hh, wv=wv, wpv=wpv: e.tensor_copy(
                            out=wpv[:, :, :, hh, :], in_=wv[:, :, :, 1 - hh, :]), rd=[w], wr=[wp])
                for (t0, tn) in blocks:
                    tts = [("hT", t0 // 128 + i) for i in range(tn // 128)]
                    for j in range(4):
                        ps = nxt("ps", self.ps[0:6])
                        for k in range(8):
                            em.op("pe", lambda e, ps=ps, k=k, j=j, t0=t0, tn=tn, w=w: e.matmul(
                                ps[:, 0:tn], lhsT=w[:, k, j * 128:(j + 1) * 128], rhs=hT[:, k, t0:t0 + tn],
                                start=(k == 0), stop=(k == 7)), rd=[w] + tts, wr=[ps])
                        ob = nxt("B", stB)
                        if rope and t0 >= 256:
                            xs = slice(t0 - 256, t0 - 256 + tn)
                            ps2 = nxt("ps", self.ps[0:6])
                            for k in range(8):
                                em.op("pe", lambda e, ps2=ps2, k=k, j=j, t0=t0, tn=tn: e.matmul(
                                    ps2[:, 0:tn], lhsT=wp[:, k, j * 128:(j + 1) * 128], rhs=hT[:, k, t0:t0 + tn],
                                    start=(k == 0), stop=(k == 7)), rd=[wp] + tts, wr=[ps2])
                            t1 = nxt("F", stF)
                            t2 = nxt("F", stF)
                            em.op("dve", lambda e, t1=t1, ps=ps, xs=xs, tn=tn: e.tensor_tensor(
                                out=t1[:, 0:tn], in0=ps[:, 0:tn], in1=cosT[:, xs], op=ALU.mult),
                                rd=[ps, cosT], wr=[t1])
                            em.op("dve", lambda e, t2=t2, ps2=ps2, xs=xs, tn=tn: e.tensor_tensor(
                                out=t2[:, 0:tn], in0=ps2[:, 0:tn], in1=sinT[:, xs], op=ALU.mult),
                                rd=[ps2, sinT], wr=[t2])
                            em.op("dve", lambda e, ob=ob, t1=t1, t2=t2, tn=tn: e.tensor_tensor(
                                out=ob[:, 0:tn], in0=t1[:, 0:tn], in1=t2[:, 0:tn], op=ALU.add),
                                rd=[t1, t2], wr=[ob])
                        else:
                            copy_op(alt(), ob[:, 0:tn], ps[:, 0:tn], [ps], [ob])
                        em.dma("sp", dst[j * 128:(j + 1) * 128, t0:t0 + tn], ob[:, 0:tn], rd=[ob],
                               wr=[(dof, j, t0)])
                continue
            for tt in range(NT):
                ps = nxt("ps", self.ps[0:6])
                rows = slice(tt * 128, (tt + 1) * 128)
                for k in range(8):
                    em.op("pe", lambda e, ps=ps, k=k, tt=tt, n=n, w=w: e.matmul(
                        ps[:, 0:n], lhsT=hT[:, k, tt * 128:(tt + 1) * 128], rhs=w[:, k, 0:n],
                        start=(k == 0), stop=(k == 7)), rd=[w, ("hT", tt)], wr=[ps])
                if kind == "rw":
                    ob = nxt("F", stF)
                    copy_op(alt(), ob[:, 0:n], ps[:, 0:n], [ps], [ob])
                    em.dma("sp", self.z_rw[rows, dof:dof + n], ob[:, 0:n], rd=[ob], wr=[("z_rw", tt, dof)])
                elif kind == "gate":
                    sgt = nxt("F", stF)
                    ob = nxt("B", stB)
                    em.op("act", lambda e, sgt=sgt, ps=ps: e.activation(out=sgt[:, :], in_=ps[:, :], func=AF.Sigmoid),
                          rd=[ps], wr=[sgt])
                    em.op("dve", lambda e, sgt=sgt, ps=ps, ob=ob: e.tensor_tensor(
                        out=ob[:, :], in0=ps[:, :], in1=sgt[:, :], op=ALU.mult), rd=[ps, sgt], wr=[ob])
                    em.dma("sp", self.zg[rows, dof:dof + 512], ob[:, :], rd=[ob], wr=[("zg", tt, dof)])
                elif kind == "mg":
                    ob = nxt("B", stB)
                    em.op("act", lambda e, ps=ps, ob=ob: e.activation(out=ob[:, :], in_=ps[:, :], func=AF.Sigmoid),
                          rd=[ps], wr=[ob])
                    em.dma("sp", self.zmg[rows, dof:dof + 512], ob[:, :], rd=[ob], wr=[("zmg", tt, dof)])
                elif kind == "nav":
                    ob = nxt("V", stV)
                    copy_op(alt(), ob[:, :, 0:64], ps[:, :].rearrange("p (h d) -> p h d", h=8), [ps], [ob])
                    em.dma("sp", self.z_nav[rows, :], ob[:, :, :].rearrange("p h d -> p (h d)"), rd=[ob],
                           wr=[("z_nav", tt)])
                elif kind == "dfv":
                    ob = nxt("DV", stDV)
                    copy_op(alt(), ob[:, :, 0:128], ps[:, :].rearrange("p (h d) -> p h d", h=4), [ps], [ob])
                    em.dma("sp", self.z_dfv[rows, :], ob[:, :, :].rearrange("p h d -> p (h d)"), rd=[ob],
                           wr=[("z_dfv", tt)])
        em.barrier()
        em.release(m0)

    def phaseR1(self, l):
        em = self.em
        m0 = em.mark()
        ident = em.tile([128, 128], F32, "ident")
        em.dma("sp", ident[:, :], self.ident_d[:, :], wr=[ident])
        mub = [em.tile([128, 1792], F32, "mub") for _ in range(2)]
        for i in range(2):
            em.dma("sp", mub[i][:, :], self.mu[l, i].partition_broadcast(128), wr=[mub[i]])
        pv = em.tile([128, 5, 512], F32, "pv")
        for i in range(5):
            em.dma("sp", pv[:, i, :], self.pvec[l, i].partition_broadcast(128), wr=[pv])
        omka = em.tile([128, 512], F32, "omka")
        em.op("dve", lambda e: e.tensor_scalar(out=omka[:, :], in0=pv[:, 1, :], scalar1=-1.0, scalar2=1.0,
                                               op0=ALU.mult, op1=ALU.add), rd=[pv], wr=[omka])
        w0b = em.tile([128, 2, 512], F32, "w0b")
        a0b = em.tile([128, 2, 512], F32, "a0b")
        wup = em.tile([64, 2, 512], F32, "wup")
        aup = em.tile([64, 2, 512], F32, "aup")
        for d in range(2):
            em.dma("sp", w0b[:, d, :], self.w0[l, d].partition_broadcast(128), wr=[w0b])
            em.dma("sp", a0b[:, d, :], self.a0[l, d].partition_broadcast(128), wr=[a0b])
            em.dma("sp", wup[:, d, :], self.w_up[l, d], wr=[wup])
            em.dma("sp", aup[:, d, :], self.a_up[l, d], wr=[aup])
        NB = 2
        zc = [em.tile([128, 1792], F32, "zc") for _ in range(NB)]
        zp = [em.tile([128, 1792], F32, "zp") for _ in range(NB)]
        zn = [em.tile([128, 1792], F32, "zn") for _ in range(NB)]
        kk = [em.tile([128, 512], F32, "kk") for _ in range(NB)]
        tmp = [em.tile([128, 512], F32, "tmp") for _ in range(3)]
        sm = [em.tile([128, 8, 4], F32, "sm") for _ in range(NB)]
        outs = [[em.tile([128, 512], F32, "o%d" % i) for i in range(6)] for _ in range(NB)]
        th = [em.tile([128, 128], F32, "th") for _ in range(2)]
        thT = [em.tile([64, 2, 128], F32, "thT") for _ in range(2)]
        bsum = [em.tile([128, 2, 8], F32, "bsum") for _ in range(NB)]
        v3 = lambda t: t[:, :].rearrange("p (h j) -> p h j", h=8)
        for tt in range(NT):
            b_ = tt % NB
            zc_, zp_, zn_, kk_, sm_, o_ = zc[b_], zp[b_], zn[b_], kk[b_], sm[b_], outs[b_]
            r0 = tt * 128
            em.dma("sp", zc_[:, :], self.z_rw[r0:r0 + 128, :], rd=[("z_rw", tt, c) for c in (0, 512, 1024, 1536)], wr=[zc_])
            first = tt in (0, 2)
            last = tt in (1, NT - 1)
            rdp = [("z_rw", t_, c) for t_ in (tt - 1, tt) if t_ >= 0 for c in (0, 512, 1024, 1536)]
            rdn = [("z_rw", t_, c) for t_ in (tt, tt + 1) if t_ < NT for c in (0, 512, 1024, 1536)]
            if first:
                em.op("pool", lambda e, zp_=zp_: e.memset(zp_[:, :], 0.0), wr=[zp_])
                em.dma("sp", zp_[1:128, :], self.z_rw[r0:r0 + 127, :], rd=rdp, wr=[zp_])
            else:
                em.dma("sp", zp_[:, :], self.z_rw[r0 - 1:r0 + 127, :], rd=rdp, wr=[zp_])
            if last:
                em.op("pool", lambda e, zn_=zn_: e.memset(zn_[:, :], 0.0), wr=[zn_])
                em.dma("sp", zn_[0:127, :], self.z_rw[r0 + 1:r0 + 128, :], rd=rdn, wr=[zn_])
            else:
                em.dma("sp", zn_[:, :], self.z_rw[r0 + 1:r0 + 129, :], rd=rdn, wr=[zn_])
            em.op("pool", lambda e, zp_=zp_, zc_=zc_: e.tensor_tensor(out=zp_[:, :], in0=zp_[:, :], in1=zc_[:, :], op=ALU.subtract), rd=[zp_, zc_], wr=[zp_])
            em.op("pool", lambda e, zn_=zn_, zc_=zc_: e.tensor_tensor(out=zn_[:, :], in0=zn_[:, :], in1=zc_[:, :], op=ALU.subtract), rd=[zn_, zc_], wr=[zn_])
            em.op("dve", lambda e, zp_=zp_: e.tensor_tensor(out=zp_[:, :], in0=zp_[:, :], in1=mub[0][:, :], op=ALU.mult), rd=[zp_, mub[0]], wr=[zp_])
            em.op("pool", lambda e, zn_=zn_: e.tensor_tensor(out=zn_[:, :], in0=zn_[:, :], in1=mub[1][:, :], op=ALU.mult), rd=[zn_, mub[1]], wr=[zn_])
            em.op("dve", lambda e, zp_=zp_, zc_=zc_: e.tensor_tensor(out=zc_[:, :], in0=zc_[:, :], in1=zp_[:, :], op=ALU.add), rd=[zp_, zc_], wr=[zc_])
            em.op("dve", lambda e, zn_=zn_, zc_=zc_: e.tensor_tensor(out=zc_[:, :], in0=zc_[:, :], in1=zn_[:, :], op=ALU.add), rd=[zn_, zc_], wr=[zc_])
            r_, k_, v_ = zc_[:, 0:512], zc_[:, 512:1024], zc_[:, 1024:1536]
            t0_, t1_, t2_ = tmp
            em.op("dve", lambda e, kk_=kk_, k_=k_: e.tensor_tensor(out=kk_[:, :], in0=k_, in1=pv[:, 0, :], op=ALU.mult), rd=[zc_, pv], wr=[kk_])
            em.op("act", lambda e, kk_=kk_: e.activation(out=t0_[:, :], in_=kk_[:, :], func=AF.Square), rd=[kk_], wr=[t0_])
            em.op("dve", lambda e, sm_=sm_: e.tensor_reduce(out=sm_[:, :, 0], in_=v3(t0_), axis=AX.X, op=ALU.add), rd=[t0_], wr=[sm_])
            em.op("act", lambda e, sm_=sm_: e.activation(out=sm_[:, :, 1], in_=sm_[:, :, 0], func=AF.Sqrt), rd=[sm_], wr=[sm_])
            em.op("dve", lambda e, sm_=sm_: e.tensor_scalar_max(out=sm_[:, :, 2], in0=sm_[:, :, 1], scalar1=1e-12), rd=[sm_], wr=[sm_])
            em.op("dve", lambda e, sm_=sm_: e.reciprocal(out=sm_[:, :, 3], in_=sm_[:, :, 2]), rd=[sm_], wr=[sm_])
            em.op("dve", lambda e, kk_=kk_, sm_=sm_: e.tensor_tensor(
                out=v3(kk_), in0=v3(kk_), in1=sm_[:, :, 3:4].to_broadcast([128, 8, 64]), op=ALU.mult), rd=[kk_, sm_], wr=[kk_])
            th0, th1 = th
            em.op("act", lambda e, zc_=zc_: e.activation(out=th0[:, :], in_=zc_[:, 1536:1664], func=AF.Tanh), rd=[zc_], wr=[th0])
            em.op("pool", lambda e, zc_=zc_: e.tensor_copy(out=th1[:, :], in_=zc_[:, 1664:1792]), rd=[zc_], wr=[th1])
            for i, (src, dstT) in enumerate(((th0, thT[0]), (th1, thT[1]))):
                ps = self.ps[5]
                for d in range(2):
                    em.op("pe", lambda e, ps=ps, src=src, d=d: e.transpose(
                        out=ps[0:64, d * 128:(d + 1) * 128], in_=src[:, d * 64:(d + 1) * 64], identity=ident[:, :]),
                        rd=[src, ident], wr=[ps])
                em.op("act", lambda e, ps=ps, dstT=dstT: e.activation(
                    out=dstT[:, :, :].rearrange("p a b -> p (a b)"), in_=ps[0:64, 0:256], func=AF.Copy), rd=[ps], wr=[dstT])
            for d in range(2):
                lw_, kd_, bb_ = o_[3 * d], o_[3 * d + 1], o_[3 * d + 2]
                psw, psa = self.ps[3], self.ps[4]
                em.op("pe", lambda e, d=d, psw=psw: e.matmul(psw[:, :], lhsT=thT[0][:, d, :], rhs=wup[:, d, :], start=True, stop=True),
                      rd=[thT[0], wup], wr=[psw])
                em.op("pe", lambda e, d=d, psa=psa: e.matmul(psa[:, :], lhsT=thT[1][:, d, :], rhs=aup[:, d, :], start=True, stop=True),
                      rd=[thT[1], aup], wr=[psa])
                em.op("dve", lambda e, d=d, psw=psw: e.tensor_tensor(out=t0_[:, :], in0=psw[:, :], in1=w0b[:, d, :], op=ALU.add), rd=[psw, w0b], wr=[t0_])
                em.op("act", lambda e: e.activation(out=t0_[:, :], in_=t0_[:, :], func=AF.Sigmoid), rd=[t0_], wr=[t0_])
                em.op("pool", lambda e, lw_=lw_: e.tensor_scalar(out=lw_[:, :], in0=t0_[:, :], scalar1=-0.6065306597126334, scalar2=0.0, op0=ALU.mult, op1=ALU.add), rd=[t0_], wr=[lw_])
                em.op("dve", lambda e, d=d, psa=psa: e.tensor_tensor(out=t1_[:, :], in0=psa[:, :], in1=a0b[:, d, :], op=ALU.add), rd=[psa, a0b], wr=[t1_])
                em.op("act", lambda e: e.activation(out=t1_[:, :], in_=t1_[:, :], func=AF.Sigmoid), rd=[t1_], wr=[t1_])
                em.op("pool", lambda e, bb_=bb_, kk_=kk_: e.tensor_tensor(out=bb_[:, :], in0=kk_[:, :], in1=t1_[:, :], op=ALU.mult), rd=[kk_, t1_], wr=[bb_])
                em.op("dve", lambda e: e.tensor_tensor(out=t2_[:, :], in0=t1_[:, :], in1=pv[:, 1, :], op=ALU.mult), rd=[t1_, pv], wr=[t2_])
                em.op("dve", lambda e: e.tensor_tensor(out=t2_[:, :], in0=t2_[:, :], in1=omka[:, :], op=ALU.add), rd=[t2_, omka], wr=[t2_])
                em.op("dve", lambda e, kd_=kd_, k_=k_: e.tensor_tensor(out=kd_[:, :], in0=t2_[:, :], in1=k_, op=ALU.mult), rd=[t2_, zc_], wr=[kd_])
                em.op("pool", lambda e, kd_=kd_: e.tensor_tensor(out=t2_[:, :], in0=kd_[:, :], in1=pv[:, 4, :], op=ALU.mult), rd=[kd_, pv], wr=[t2_])
                em.op("pool", lambda e, r_=r_: e.tensor_tensor(out=t2_[:, :], in0=t2_[:, :], in1=r_, op=ALU.mult), rd=[t2_, zc_], wr=[t2_])
                em.op("dve", lambda e, d=d, bs=bsum[b_]: e.tensor_reduce(out=bs[:, d, :], in_=v3(t2_), axis=AX.X, op=ALU.add), rd=[t2_], wr=[bsum[b_]])
                for i, t_ in enumerate((lw_, kd_, bb_)):
                    em.dma("sp", self.rwp[3 + 3 * d + i, r0:r0 + 128, :], t_[:, :], rd=[t_], wr=[("rwp", 3 + 3 * d + i, tt)])
            em.op("dve", lambda e, bs=bsum[b_]: e.tensor_tensor(out=bs[:, 0, :], in0=bs[:, 0, :], in1=bs[:, 1, :], op=ALU.add), rd=[bsum[b_]], wr=[bsum[b_]])
            em.dma("sp", self.rwbs[r0:r0 + 128, :], bsum[b_][:, 0, :], rd=[bsum[b_]], wr=[("rwbs", tt)])
            em.dma("sp", self.rwp[0, r0:r0 + 128, :], r_, rd=[zc_], wr=[("rwp", 0, tt)])
            em.dma("sp", self.rwp[1, r0:r0 + 128, :], v_, rd=[zc_], wr=[("rwp", 1, tt)])
            em.dma("sp", self.rwp[2, r0:r0 + 128, :], kk_[:, :], rd=[kk_], wr=[("rwp", 2, tt)])
        em.barrier()
        em.release(m0)

    def phaseR2(self, l):
        em = self.em
        m0 = em.mark()
        PS = self.ps
        ident = em.tile([128, 128], F32, "ident")
        em.dma("sp", ident[:, :], self.ident_d[:, :], wr=[ident])
        id64 = ident[0:64, 0:64]
        tri = em.tile([64, 2, 64], F32, "tri")
        mklt = em.tile([64, 2, 128], F32, "mklt")
        mkl = em.tile([64, 2, 64], F32, "mkl")
        for d in range(2):
            em.dma("sp", tri[:, d, :], self.tri_d[d], wr=[tri])
            em.dma("sp", mklt[:, d, :], self.mklt_d[d], wr=[mklt])
            em.dma("sp", mkl[:, d, :], self.mkl_d[d], wr=[mkl])
        ones = em.tile([64, 64], F32, "ones")
        em.op("pool", lambda e: e.memset(ones[:, :], 1.0), wr=[ones])
        S0T = [em.tile([64, 8, 64], F32, "S0T") for _ in range(2)]
        for d in range(2):
            em.op("pool", lambda e, d=d: e.memset(S0T[d][:, :, :], 0.0), wr=[S0T[d]])
        ld = [[em.tile([64, 512], F32, "ld") for _ in range(5)] for _ in range(2)]
        clsb = em.tile([64, 512], F32, "clsb")
        tmpe = [em.tile([64, 512], F32, "tmpe") for _ in range(2)]
        qm = [em.tile([64, 512], F32, "qm") for _ in range(4)]
        bT = em.tile([64, 8, 64], F32, "bT")
        kT = em.tile([64, 8, 64], F32, "kT")
        X = [em.tile([64, 8, 64], F32, "X") for _ in range(2)]
        XT = [em.tile([64, 8, 64], F32, "XT") for _ in range(2)]
        Wsb = [em.tile([64, 8, 64], F32, "Wsb") for _ in range(2)]
        Usb = [em.tile([64, 8, 64], F32, "Usb") for _ in range(2)]
        Ysb = [em.tile([64, 512], F32, "Ysb") for _ in range(2)]

        class Set:
            pass
        sets = [[Set() for _ in range(2)] for _ in range(2)]
        for d in range(2):
            for p in range(2):
                s_ = sets[d][p]
                s_.v = em.tile([64, 512], F32, "sv")
                s_.bh = em.tile([64, 512], F32, "sbh")
                s_.kh = em.tile([64, 512], F32, "skh")
                s_.AR = em.tile([64, 8, 2, 64], F32, "sAR")
                s_.Lb = em.tile([64, 8, 128], F32, "sLb")
                s_.Lk = em.tile([64, 8, 128], F32, "sLk")
                s_.TT = em.tile([64, 8, 64], F32, "sTT")
                s_.pct = em.tile([64, 8], F32, "spct")
        order = [list(range(68)), [3, 2, 1, 0] + list(range(67, 3, -1))]
        h4 = lambda t: t[:, :].rearrange("p (h j) -> p h j", h=8)
        nld = [0]

        def prep(d, c, s_):
            L_ = ld[nld[0] % 2]
            nld[0] += 1
            r_, kk_, lw_, kd_, b_ = L_
            rows = slice(c * 64, (c + 1) * 64)
            tt = c // 2
            for t_, idx in ((r_, 0), (kk_, 2), (lw_, 3 + 3 * d), (kd_, 4 + 3 * d), (b_, 5 + 3 * d)):
                em.dma("sp", t_[:, :], self.rwp[idx, rows, :], rd=[("rwp", idx, tt)], wr=[t_])
            em.dma("sp", s_.v[:, :], self.rwp[1, rows, :], rd=[("rwp", 1, tt)], wr=[s_.v])
            em.op("pe", lambda e: e.matmul(PS[0][0:64, :], lhsT=tri[:, d, :], rhs=lw_[:, :], start=True, stop=True), rd=[tri, lw_], wr=[PS[0]])
            em.op("pe", lambda e: e.matmul(PS[1][0:64, :], lhsT=ones[:, :], rhs=lw_[:, :], start=True, stop=True), rd=[ones, lw_], wr=[PS[1]])
            for h in range(8):
                em.op("pe", lambda e, h=h: e.matmul(PS[6][0:64, h:h + 1], lhsT=lw_[:, h * 64:(h + 1) * 64], rhs=ones[:, 0:1],
                                                    start=True, stop=True), rd=[lw_, ones], wr=[PS[6]])
            em.op("act", lambda e: e.activation(out=s_.pct[:, :], in_=PS[6][0:64, 0:8], func=AF.Exp), rd=[PS[6]], wr=[s_.pct])
            em.op("act", lambda e: e.activation(out=clsb[:, :], in_=PS[0][0:64, :], func=AF.Copy), rd=[PS[0]], wr=[clsb])
            e0, e1 = tmpe
            em.op("dve", lambda e: e.tensor_tensor(out=e0[:, :], in0=clsb[:, :], in1=lw_[:, :], op=ALU.subtract), rd=[clsb, lw_], wr=[e0])
            em.op("act", lambda e: e.activation(out=e0[:, :], in_=e0[:, :], func=AF.Exp), rd=[e0], wr=[e0])
            em.op("dve", lambda e: e.scalar_tensor_tensor(out=qm[0][:, :], in0=kk_[:, :], scalar=-1.0, in1=e0[:, :],
                                                          op0=ALU.mult, op1=ALU.mult), rd=[kk_, e0], wr=[qm[0]])
            em.op("act", lambda e: e.activation(out=e1[:, :], in_=clsb[:, :], func=AF.Exp), rd=[clsb], wr=[e1])
            em.op("pool", lambda e: e.tensor_tensor(out=qm[1][:, :], in0=r_[:, :], in1=e1[:, :], op=ALU.mult), rd=[r_, e1], wr=[qm[1]])
            em.op("act", lambda e: e.activation(out=e0[:, :], in_=clsb[:, :], func=AF.Exp, scale=-1.0), rd=[clsb], wr=[e0])
            em.op("pool", lambda e: e.tensor_tensor(out=qm[2][:, :], in0=b_[:, :], in1=e0[:, :], op=ALU.mult), rd=[b_, e0], wr=[qm[2]])
            em.op("dve", lambda e: e.tensor_tensor(out=qm[3][:, :], in0=kd_[:, :], in1=e0[:, :], op=ALU.mult), rd=[kd_, e0], wr=[qm[3]])
            em.op("dve", lambda e: e.tensor_tensor(out=e1[:, :], in0=PS[1][0:64, :], in1=clsb[:, :], op=ALU.subtract), rd=[PS[1], clsb], wr=[e1])
            em.op("act", lambda e: e.activation(out=e1[:, :], in_=e1[:, :], func=AF.Exp), rd=[e1], wr=[e1])
            em.op("pool", lambda e: e.tensor_tensor(out=s_.bh[:, :], in0=b_[:, :], in1=e1[:, :], op=ALU.mult), rd=[b_, e1], wr=[s_.bh])
            em.op("pool", lambda e: e.tensor_tensor(out=s_.kh[:, :], in0=kd_[:, :], in1=e1[:, :], op=ALU.mult), rd=[kd_, e1], wr=[s_.kh])
            dsts = (s_.AR[:, :, 0, :], s_.AR[:, :, 1, :], bT[:, :, :], kT[:, :, :])
            dkeys = (s_.AR, s_.AR, bT, kT)
            for qi in range(4):
                for h in range(8):
                    em.op("pe", lambda e, qi=qi, h=h: e.transpose(
                        out=PS[qi][0:64, h * 64:(h + 1) * 64], in_=qm[qi][:, h * 64:(h + 1) * 64], identity=id64),
                        rd=[qm[qi], ident], wr=[PS[qi]])
                src = PS[qi][0:64, :].rearrange("p (h t) -> p h t", h=8)
                if qi % 2 == 0:
                    em.op("act", lambda e, qi=qi, src=src: e.activation(out=dsts[qi], in_=src, func=AF.Copy), rd=[PS[qi]], wr=[dkeys[qi]])
                else:
                    em.op("dve", lambda e, qi=qi, src=src: e.tensor_copy(out=dsts[qi], in_=src), rd=[PS[qi]], wr=[dkeys[qi]])
            for (srcT, dstL, b0) in ((bT, s_.Lb, 0), (kT, s_.Lk, 2)):
                for h in range(8):
                    em.op("pe", lambda e, h=h, srcT=srcT, b0=b0: e.matmul(
                        PS[b0 + h // 4][0:64, (h % 4) * 128:(h % 4 + 1) * 128], lhsT=srcT[:, h, :],
                        rhs=s_.AR[:, h, :, :].rearrange("p a t -> p (a t)"), start=True, stop=True),
                        rd=[srcT, s_.AR], wr=[PS[b0 + h // 4]])
                for half in range(2):
                    em.op("dve", lambda e, half=half, dstL=dstL, b0=b0: e.tensor_tensor(
                        out=dstL[:, half * 4:(half + 1) * 4, :],
                        in0=PS[b0 + half][0:64, :].rearrange("p (h t) -> p h t", h=4),
                        in1=mklt[:, d, :].unsqueeze(1).to_broadcast([64, 4, 128]), op=ALU.mult),
                        rd=[PS[b0 + half], mklt], wr=[dstL])
            for h in range(8):
                em.op("pe", lambda e, h=h: e.matmul(PS[4][0:64, h * 64:(h + 1) * 64], lhsT=s_.AR[:, h, 0, :], rhs=bT[:, h, :],
                                                    start=True, stop=True), rd=[s_.AR, bT], wr=[PS[4]])
            em.op("dve", lambda e: e.tensor_tensor(out=X[0][:, :, :], in0=PS[4][0:64, :].rearrange("p (h t) -> p h t", h=8),
                                                   in1=mkl[:, d, :].unsqueeze(1).to_broadcast([64, 8, 64]), op=ALU.mult),
                  rd=[PS[4], mkl], wr=[X[0]])
            em.op("pool", lambda e: e.tensor_copy(out=XT[0][:, :, :], in_=s_.Lb[:, :, 0:64]), rd=[s_.Lb], wr=[XT[0]])
            em.op("pool", lambda e: e.tensor_tensor(out=s_.TT[:, :, :], in0=s_.Lb[:, :, 0:64],
                                                    in1=id64.unsqueeze(1).to_broadcast([64, 8, 64]), op=ALU.add),
                  rd=[s_.Lb, ident], wr=[s_.TT])
            for k in range(1, 6):
                cur, nx = (k - 1) % 2, k % 2
                for h in range(8):
                    em.op("pe", lambda e, h=h, cur=cur: e.matmul(PS[0][0:64, h * 64:(h + 1) * 64], lhsT=XT[cur][:, h, :], rhs=X[cur][:, h, :],
                                                                 start=True, stop=True), rd=[XT[cur], X[cur]], wr=[PS[0]])
                if k < 5:
                    for h in range(8):
                        em.op("pe", lambda e, h=h, cur=cur: e.matmul(PS[1][0:64, h * 64:(h + 1) * 64], lhsT=X[cur][:, h, :], rhs=XT[cur][:, h, :],
                                                                     start=True, stop=True), rd=[XT[cur], X[cur]], wr=[PS[1]])
                em.op("act", lambda e, nx=nx: e.activation(out=X[nx][:, :, :].rearrange("p h t -> p (h t)"), in_=PS[0][0:64, :], func=AF.Copy),
                      rd=[PS[0]], wr=[X[nx]])
                if k < 5:
                    em.op("dve", lambda e, nx=nx: e.tensor_copy(out=XT[nx][:, :, :].rearrange("p h t -> p (h t)"), in_=PS[1][0:64, :]),
                          rd=[PS[1]], wr=[XT[nx]])
                for h in range(8):
                    em.op("pe", lambda e, h=h, nx=nx: e.matmul(PS[2][0:64, h * 64:(h + 1) * 64], lhsT=X[nx][:, h, :], rhs=s_.TT[:, h, :],
                                                               start=True, stop=True), rd=[X[nx], s_.TT], wr=[PS[2]])
                em.op("dve", lambda e: e.tensor_tensor(out=s_.TT[:, :, :].rearrange("p h t -> p (h t)"),
                                                       in0=s_.TT[:, :, :].rearrange("p h t -> p (h t)"), in1=PS[2][0:64, :], op=ALU.add),
                      rd=[s_.TT, PS[2]], wr=[s_.TT])

        def chain(d, c, s_):
            St = S0T[d]
            W_, U_, Y_ = Wsb[d], Usb[d], Ysb[d]
            hs = lambda h: slice(h * 64, (h + 1) * 64)
            for h in range(8):
                em.op("pe", lambda e, h=h: e.matmul(PS[5][0:64, hs(h)], lhsT=s_.AR[:, h, 0, :], rhs=St[:, h, :], start=True, stop=False),
                      rd=[s_.AR, St], wr=[PS[5]])
                em.op("pe", lambda e, h=h: e.matmul(PS[5][0:64, hs(h)], lhsT=s_.Lk[:, h, 0:64], rhs=s_.v[:, hs(h)], start=False, stop=True),
                      rd=[s_.Lk, s_.v], wr=[PS[5]])
            em.op("act", lambda e: e.activation(out=W_[:, :, :].rearrange("p h t -> p (h t)"), in_=PS[5][0:64, :], func=AF.Copy), rd=[PS[5]], wr=[W_])
            for h in range(8):
                em.op("pe", lambda e, h=h: e.matmul(PS[6][0:64, hs(h)], lhsT=s_.TT[:, h, :], rhs=W_[:, h, :], start=True, stop=True),
                      rd=[s_.TT, W_], wr=[PS[6]])
            em.op("dve", lambda e: e.tensor_copy(out=U_[:, :, :].rearrange("p h t -> p (h t)"), in_=PS[6][0:64, :]), rd=[PS[6]], wr=[U_])
            for h in range(8):
                em.op("pe", lambda e, h=h: e.matmul(PS[5][0:64, hs(h)], lhsT=s_.AR[:, h, 1, :], rhs=St[:, h, :], start=True, stop=False),
                      rd=[s_.AR, St], wr=[PS[5]])
                em.op("pe", lambda e, h=h: e.matmul(PS[5][0:64, hs(h)], lhsT=s_.Lb[:, h, 64:128], rhs=U_[:, h, :], start=False, stop=False),
                      rd=[s_.Lb, U_], wr=[PS[5]])
                em.op("pe", lambda e, h=h: e.matmul(PS[5][0:64, hs(h)], lhsT=s_.Lk[:, h, 64:128], rhs=s_.v[:, hs(h)], start=False, stop=True),
                      rd=[s_.Lk, s_.v], wr=[PS[5]])
            em.op("act", lambda e: e.activation(out=Y_[:, :], in_=PS[5][0:64, :], func=AF.Copy), rd=[PS[5]], wr=[Y_])
            em.dma("sp", self.ysc[d, c * 64:(c + 1) * 64, :], Y_[:, :], rd=[Y_], wr=[("ysc", d, c)])
            for h in range(8):
                em.op("pe", lambda e, h=h: e.matmul(PS[6][0:64, hs(h)], lhsT=s_.bh[:, hs(h)], rhs=U_[:, h, :], start=True, stop=False),
                      rd=[s_.bh, U_], wr=[PS[6]])
                em.op("pe", lambda e, h=h: e.matmul(PS[6][0:64, hs(h)], lhsT=s_.kh[:, hs(h)], rhs=s_.v[:, hs(h)], start=False, stop=True),
                      rd=[s_.kh, s_.v], wr=[PS[6]])
            em.op("dve", lambda e: e.tensor_tensor(out=St[:, :, :], in0=St[:, :, :], in1=s_.pct[:, :].unsqueeze(2).to_broadcast([64, 8, 64]),
                                                   op=ALU.mult), rd=[St, s_.pct], wr=[St])
            em.op("dve", lambda e: e.tensor_tensor(out=St[:, :, :].rearrange("p h t -> p (h t)"), in0=St[:, :, :].rearrange("p h t -> p (h t)"),
                                                   in1=PS[6][0:64, :], op=ALU.add), rd=[St, PS[6]], wr=[St])

        nsteps = len(order[0])
        for step in range(nsteps + 1):
            for d in range(2):
                if step < nsteps:
                    prep(d, order[d][step], sets[d][step % 2])
            for d in range(2):
                if step >= 1:
                    chain(d, order[d][step - 1], sets[d][(step - 1) % 2])
        em.barrier()
        em.release(m0)

    def phaseR3(self, l):
        em = self.em
        m0 = em.mark()
        pv = em.tile([128, 5, 512], F32, "pv")
        for i in range(5):
            em.dma("sp", pv[:, i, :], self.pvec[l, i].partition_broadcast(128), wr=[pv])
        NB = 2
        yf = [em.tile([128, 512], F32, "yf") for _ in range(NB)]
        yb = [em.tile([128, 512], F32, "yb") for _ in range(NB)]
        vv = [em.tile([128, 512], F32, "vv") for _ in range(NB)]
        sq = [em.tile([128, 512], F32, "sq") for _ in range(NB)]
        gz = [em.tile([128, 512], BF16, "gz") for _ in range(NB)]
        ob = [em.tile([128, 512], BF16, "ob") for _ in range(NB)]
        bs = [em.tile([128, 8], F32, "bs") for _ in range(NB)]
        sm = [em.tile([128, 8, 4], F32, "sm") for _ in range(NB)]
        v3 = lambda t: t[:, :].rearrange("p (h j) -> p h j", h=8)
        ntile = NT if l < L - 1 else NT
        for tt in range(ntile):
            b_ = tt % NB
            y_, y2, v_, s_, g_, o_, bs_, sm_ = yf[b_], yb[b_], vv[b_], sq[b_], gz[b_], ob[b_], bs[b_], sm[b_]
            r0 = tt * 128
            ykeys = lambda d: [("ysc", d, 2 * tt), ("ysc", d, 2 * tt + 1)]
            em.dma("sp", y_[:, :], self.ysc[0, r0:r0 + 128, :], rd=ykeys(0), wr=[y_])
            em.dma("sp", y2[:, :], self.ysc[1, r0:r0 + 128, :], rd=ykeys(1), wr=[y2])
            em.dma("sp", v_[:, :], self.rwp[1, r0:r0 + 128, :], rd=[("rwp", 1, tt)], wr=[v_])
            em.dma("sp", bs_[:, :], self.rwbs[r0:r0 + 128, :], rd=[("rwbs", tt)], wr=[bs_])
            em.dma("sp", g_[:, :], self.zg[r0:r0 + 128, 0:512], rd=[("zg", tt, 0)], wr=[g_])
            em.op("dve", lambda e, y_=y_, y2=y2: e.tensor_tensor(out=y_[:, :], in0=y_[:, :], in1=y2[:, :], op=ALU.add), rd=[y_, y2], wr=[y_])
            em.op("dve", lambda e, y_=y_, sm_=sm_: e.tensor_reduce(out=sm_[:, :, 0], in_=v3(y_), axis=AX.X, op=ALU.add), rd=[y_], wr=[sm_])
            em.op("dve", lambda e, sm_=sm_: e.tensor_scalar(out=sm_[:, :, 0], in0=sm_[:, :, 0], scalar1=1.0 / 64, scalar2=0.0, op0=ALU.mult, op1=ALU.add),
                  rd=[sm_], wr=[sm_])
            em.op("dve", lambda e, y_=y_, sm_=sm_: e.tensor_tensor(out=v3(y_), in0=v3(y_), in1=sm_[:, :, 0:1].to_broadcast([128, 8, 64]), op=ALU.subtract),
                  rd=[y_, sm_], wr=[y_])
            em.op("act", lambda e, y_=y_, s_=s_: e.activation(out=s_[:, :], in_=y_[:, :], func=AF.Square), rd=[y_], wr=[s_])
            em.op("dve", lambda e, s_=s_, sm_=sm_: e.tensor_reduce(out=sm_[:, :, 1], in_=v3(s_), axis=AX.X, op=ALU.add), rd=[s_], wr=[sm_])
            em.op("act", lambda e, sm_=sm_: e.activation(out=sm_[:, :, 2], in_=sm_[:, :, 1], func=AF.Sqrt, bias=64e-5, scale=1.0 / 64), rd=[sm_], wr=[sm_])
            em.op("dve", lambda e, sm_=sm_: e.reciprocal(out=sm_[:, :, 3], in_=sm_[:, :, 2]), rd=[sm_], wr=[sm_])
            em.op("dve", lambda e, y_=y_, sm_=sm_: e.tensor_tensor(out=v3(y_), in0=v3(y_), in1=sm_[:, :, 3:4].to_broadcast([128, 8, 64]), op=ALU.mult),
                  rd=[y_, sm_], wr=[y_])
            em.op("pool", lambda e, y_=y_: e.tensor_tensor(out=y_[:, :], in0=y_[:, :], in1=pv[:, 2, :], op=ALU.mult), rd=[y_, pv], wr=[y_])
            em.op("pool", lambda e, y_=y_: e.tensor_tensor(out=y_[:, :], in0=y_[:, :], in1=pv[:, 3, :], op=ALU.add), rd=[y_, pv], wr=[y_])
            em.op("pool", lambda e, v_=v_, bs_=bs_: e.tensor_tensor(out=v3(v_), in0=v3(v_), in1=bs_[:, :].unsqueeze(2).to_broadcast([128, 8, 64]), op=ALU.mult),
                  rd=[v_, bs_], wr=[v_])
            em.op("dve", lambda e, y_=y_, v_=v_: e.tensor_tensor(out=y_[:, :], in0=y_[:, :], in1=v_[:, :], op=ALU.add), rd=[y_, v_], wr=[y_])
            em.op("dve", lambda e, y_=y_, g_=g_, o_=o_: e.tensor_tensor(out=o_[:, :], in0=y_[:, :], in1=g_[:, :], op=ALU.mult), rd=[y_, g_], wr=[o_])
            em.dma("sp", self.og[r0:r0 + 128, 0:512], o_[:, :], rd=[o_], wr=[("og", tt, 0)])
        em.barrier()
        em.release(m0)

    def phaseC(self, l):
        import math
        em = self.em
        m0 = em.mark()
        PS = self.ps
        lam_init = 0.8 - 0.6 * math.exp(-0.3 * l)
        lq = em.tile([128, 128], F32, "lq")
        lk = em.tile([128, 128], F32, "lk")
        em.dma("sp", lq[:, :], self.lamq[l].rearrange("a d -> (a d)").partition_broadcast(128), wr=[lq])
        em.dma("sp", lk[:, :], self.lamk[l].rearrange("a d -> (a d)").partition_broadcast(128), wr=[lk])
        lam = em.tile([128, 4], F32, "lam")
        em.op("dve", lambda e: e.tensor_tensor(out=lq[:, :], in0=lq[:, :], in1=lk[:, :], op=ALU.mult), rd=[lq, lk], wr=[lq])
        em.op("dve", lambda e: e.tensor_reduce(out=lam[:, 0:2], in_=lq[:, :].rearrange("p (a d) -> p a d", a=2), axis=AX.X, op=ALU.add), rd=[lq], wr=[lam])
        em.op("act", lambda e: e.activation(out=lam[:, 0:2], in_=lam[:, 0:2], func=AF.Exp), rd=[lam], wr=[lam])
        em.op("dve", lambda e: e.tensor_tensor(out=lam[:, 2:3], in0=lam[:, 0:1], in1=lam[:, 1:2], op=ALU.subtract), rd=[lam], wr=[lam])
        em.op("dve", lambda e: e.tensor_scalar(out=lam[:, 3:4], in0=lam[:, 2:3], scalar1=-1.0, scalar2=-lam_init, op0=ALU.mult, op1=ALU.add), rd=[lam], wr=[lam])
        gsub = em.tile([128, 128], F32, "gsub")
        em.dma("sp", gsub[:, :], self.subln[l].partition_broadcast(128), wr=[gsub])
        em.op("dve", lambda e: e.tensor_scalar(out=gsub[:, :], in0=gsub[:, :], scalar1=(1.0 - lam_init), scalar2=0.0, op0=ALU.mult, op1=ALU.add), rd=[gsub], wr=[gsub])
        QT = [em.tile([128, T], BF16, "QT") for _ in range(4)]
        KT = [em.tile([128, T], BF16, "KT") for _ in range(4)]
        Vt = [em.tile([128, NT, 129], BF16, "Vt") for _ in range(4)]
        Eb = [em.tile([128, 512], BF16, "Eb") for _ in range(4)]
        o0 = [em.tile([128, 128], F32, "o0") for _ in range(4)]
        oo = [em.tile([128, 128], F32, "oo") for _ in range(4)]
        jk = em.tile([128, 128], F32, "jk")
        A0s = [em.tile([128, 132], F32, "A0s") for _ in range(4)]
        A1s = [em.tile([128, 132], F32, "A1s") for _ in range(4)]
        gt = [em.tile([128, 128], BF16, "gt") for _ in range(4)]
        ob = [em.tile([128, 128], BF16, "ob") for _ in range(4)]
        sc = [em.tile([128, 16], F32, "sc") for _ in range(4)]
        cnt = dict(e=0, s=0, ep=0)
        qblocks = [(0, 256, [0, 1])] + [(256 + 512 * i, 512, list(range(NT))) for i in range(8)]
        for h in range(4):
            Q_, K_, V_ = QT[h], KT[h], Vt[h]
            em.dma("sp", Q_[:, :], self.zT_dfq[h * 128:(h + 1) * 128, :], rd=[("dfq", hh, t0) for hh in range(4) for t0 in range(0, T, 256)][:0], wr=[Q_])
            em.dma("sp", K_[:, :], self.zT_dfk[h * 128:(h + 1) * 128, :], wr=[K_])
            for t8 in range(0, NT, 2):
                em.dma("sp", V_[:, t8:t8 + 2, :],
                       self.z_dfv[t8 * 128:(t8 + 2) * 128, h * 129:(h + 1) * 129].rearrange("(t p) c -> p t c", p=128), wr=[V_])
            for (q0, nq, ktiles) in qblocks:
                nqs = nq // 128
                for m in range(2):
                    for ki, kt in enumerate(ktiles):
                        pss = PS[cnt["s"] % 3]
                        cnt["s"] += 1
                        em.op("pe", lambda e, pss=pss, m=m, kt=kt, q0=q0, nq=nq, K_=K_, Q_=Q_: e.matmul(
                            pss[:, 0:nq], lhsT=K_[m * 64:(m + 1) * 64, kt * 128:(kt + 1) * 128], rhs=Q_[m * 64:(m + 1) * 64, q0:q0 + nq],
                            start=True, stop=True), rd=[K_, Q_], wr=[pss])
                        E_ = Eb[cnt["e"] % 4]
                        cnt["e"] += 1
                        em.op("act", lambda e, pss=pss, E_=E_, nq=nq: e.activation(out=E_[:, 0:nq], in_=pss[:, 0:nq], func=AF.Exp, scale=0.125),
                              rd=[pss], wr=[E_])
                        if ki == 0:
                            started = set()
                        for qs in range(nqs):
                            i = m * 4 + qs
                            bank = 3 + i // 3
                            acc = PS[bank]
                            c0 = (i % 3) * 129
                            st_ = (ki == 0 and bank not in started)
                            started.add(bank)
                            em.op("pe", lambda e, acc=acc, c0=c0, E_=E_, qs=qs, kt=kt, ki=ki, V_=V_, ktiles=ktiles, st_=st_: e.matmul(
                                acc[:, c0:c0 + 129], lhsT=E_[:, qs * 128:(qs + 1) * 128], rhs=V_[:, kt, :],
                                start=st_, stop=(ki == len(ktiles) - 1), skip_group_check=True), rd=[E_, V_], wr=[("acc", i)])
                for qs in range(nqs):
                    ep = cnt["ep"] % 4
                    cnt["ep"] += 1
                    s_, o0_, oo_, g_, ob_ = sc[ep], o0[ep], oo[ep], gt[ep], ob[ep]
                    r0 = q0 + qs * 128
                    a0 = PS[3 + qs // 3]
                    c0 = (qs % 3) * 129
                    i1 = 4 + qs
                    a1 = PS[3 + i1 // 3]
                    c1 = (i1 % 3) * 129
                    em.dma("sp", g_[:, :], self.zg[r0:r0 + 128, 1024 + h * 128:1024 + (h + 1) * 128], wr=[g_])
                    A0_, A1_ = A0s[ep], A1s[ep]
                    em.op("act", lambda e, A0_=A0_, a0=a0, c0=c0: e.activation(out=A0_[:, 0:129], in_=a0[:, c0:c0 + 129], func=AF.Copy),
                          rd=[("acc", qs)], wr=[A0_])
                    em.op("dve", lambda e, A1_=A1_, a1=a1, c1=c1: e.tensor_copy(out=A1_[:, 0:129], in_=a1[:, c1:c1 + 129]),
                          rd=[("acc", i1)], wr=[A1_])
                    em.op("dve", lambda e, s_=s_, A0_=A0_: e.reciprocal(out=s_[:, 0:1], in_=A0_[:, 128:129]), rd=[A0_], wr=[s_])
                    em.op("dve", lambda e, s_=s_, A1_=A1_: e.reciprocal(out=s_[:, 1:2], in_=A1_[:, 128:129]), rd=[A1_], wr=[s_])
                    em.op("dve", lambda e, s_=s_: e.tensor_tensor(out=s_[:, 2:3], in0=s_[:, 1:2], in1=lam[:, 3:4], op=ALU.mult), rd=[s_, lam], wr=[s_])
                    em.op("act", lambda e, o0_=o0_, A0_=A0_, s_=s_: e.activation(out=o0_[:, :], in_=A0_[:, 0:128], func=AF.Copy, scale=s_[:, 0:1]),
                          rd=[A0_, s_], wr=[o0_])
                    em.op("dve", lambda e, oo_=oo_, A1_=A1_, s_=s_, o0_=o0_: e.scalar_tensor_tensor(
                        out=oo_[:, :], in0=A1_[:, 0:128], scalar=s_[:, 2:3], in1=o0_[:, :], op0=ALU.mult, op1=ALU.add),
                        rd=[A1_, s_, o0_], wr=[oo_])
                    em.op("pool", lambda e, s_=s_: e.memset(s_[:, 3:4], 0.0), rd=[], wr=[s_])
                    em.op("act", lambda e, oo_=oo_, s_=s_: e.activation(out=jk[:, :], in_=oo_[:, :], func=AF.Square, accum_out=s_[:, 3:4]),
                          rd=[oo_, s_], wr=[jk, s_])
                    em.op("act", lambda e, s_=s_: e.activation(out=s_[:, 4:5], in_=s_[:, 3:4], func=AF.Sqrt, bias=1e-5, scale=1.0 / 128), rd=[s_], wr=[s_])
                    em.op("dve", lambda e, s_=s_: e.reciprocal(out=s_[:, 5:6], in_=s_[:, 4:5]), rd=[s_], wr=[s_])
                    em.op("dve", lambda e, oo_=oo_, s_=s_: e.scalar_tensor_tensor(
                        out=oo_[:, :], in0=oo_[:, :], scalar=s_[:, 5:6], in1=gsub[:, :], op0=ALU.mult, op1=ALU.mult),
                        rd=[oo_, s_, gsub], wr=[oo_])
                    em.op("dve", lambda e, oo_=oo_, g_=g_, ob_=ob_: e.tensor_tensor(out=ob_[:, :], in0=oo_[:, :], in1=g_[:, :], op=ALU.mult),
                          rd=[oo_, g_], wr=[ob_])
                    em.dma("sp", self.og[r0:r0 + 128, 1024 + h * 128:1024 + (h + 1) * 128], ob_[:, :], rd=[ob_], wr=[("og", r0, 2, h)])
        em.barrier()
        em.release(m0)

    def phaseB(self, l):
        em = self.em
        m0 = em.mark()
        PS = self.ps
        QT = [em.tile([64, T], BF16, "QT") for _ in range(2)]
        KT = [em.tile([64, T], BF16, "KT") for _ in range(2)]
        Vt = [em.tile([128, NT, 65], BF16, "Vt") for _ in range(2)]
        Gf = em.tile([128, 15 * 64], F32, "Gf")
        Gt = [em.tile([128, 15, 64], BF16, "Gt") for _ in range(2)]
        Eb = [em.tile([128, 512], BF16, "Eb") for _ in range(4)]
        gt = [em.tile([128, 64], BF16, "gt") for _ in range(2)]
        ob = [em.tile([128, 64], BF16, "ob") for _ in range(2)]
        sc = [em.tile([128, 2], F32, "sc") for _ in range(2)]
        cnt = dict(e=0, s=0, ep=0, alt=0)

        def rstart(r):
            return min(max(r - 4, 0), 56)

        for h in range(8):
            Q_, K_, V_, G_ = QT[h % 2], KT[h % 2], Vt[h % 2], Gt[h % 2]
            em.dma("sp", Q_[:, :], self.zT_naq[h * 64:(h + 1) * 64, :], wr=[Q_])
            em.dma("sp", K_[:, :], self.zT_nak[h * 64:(h + 1) * 64, :], wr=[K_])
            for t8 in range(0, NT, 2):
                em.dma("sp", V_[:, t8:t8 + 2, :],
                       self.z_nav[t8 * 128:(t8 + 2) * 128, h * 65:(h + 1) * 65].rearrange("(t p) c -> p t c", p=128), wr=[V_])
            for half in range(2):
                em.dma("sp", Gf[half * 64:(half + 1) * 64, :], self.nabias[l, h].rearrange("k a q -> k (a q)"), wr=[Gf])
            em.op("act", lambda e, G_=G_: e.activation(out=G_[:, :, :].rearrange("p a q -> p (a q)"), in_=Gf[:, :], func=AF.Exp), rd=[Gf], wr=[G_])
            for blk in range(9):
                if blk == 0:
                    q0, nq = 0, 256
                    tiles = [(0, None), (1, None)]
                else:
                    Rb = blk - 1
                    q0, nq = 256 + Rb * 512, 512
                    rlo, rhi = rstart(8 * Rb), rstart(8 * Rb + 7) + 7
                    tiles = [(0, None), (1, None)] + [(2 + a, a) for a in range(rlo // 2, rhi // 2 + 1)]
                nqs = nq // 128
                for ki, (kt, a) in enumerate(tiles):
                    pss = PS[cnt["s"] % 3]
                    cnt["s"] += 1
                    em.op("pe", lambda e, pss=pss, kt=kt, q0=q0, nq=nq, K_=K_, Q_=Q_: e.matmul(
                        pss[:, 0:nq], lhsT=K_[:, kt * 128:(kt + 1) * 128], rhs=Q_[:, q0:q0 + nq], start=True, stop=True),
                        rd=[K_, Q_], wr=[pss])
                    E_ = Eb[cnt["e"] % 4]
                    cnt["e"] += 1
                    em.op("act", lambda e, pss=pss, E_=E_, nq=nq: e.activation(out=E_[:, 0:nq], in_=pss[:, 0:nq], func=AF.Exp, scale=0.125),
                          rd=[pss], wr=[E_])
                    if a is not None:
                        Rb = blk - 1
                        for krl in range(2):
                            kr = 2 * a + krl
                            idxs = []
                            for qrl in range(8):
                                qr = 8 * Rb + qrl
                                ok = rstart(qr) <= kr < rstart(qr) + 8
                                idxs.append((7 - kr + qr) if ok else None)
                            runs = []
                            for qrl, ix in enumerate(idxs):
                                if runs and ((ix is None and runs[-1][1] is None) or
                                             (ix is not None and runs[-1][1] is not None and ix == runs[-1][1] + (qrl - runs[-1][0]))):
                                    runs[-1][2] += 1
                                else:
                                    runs.append([qrl, ix, 1])
                            for (qrl, ix, n) in runs:
                                cnt["alt"] += 1
                                eng = "dve" if cnt["alt"] % 2 else "pool"
                                dst = E_[krl * 64:(krl + 1) * 64, qrl * 64:(qrl + n) * 64]
                                if ix is None:
                                    em.op(eng, lambda e, dst=dst: e.memset(dst, 0.0), rd=[E_], wr=[E_])
                                else:
                                    assert 0 <= ix and ix + n <= 15
                                    src = G_[krl * 64:(krl + 1) * 64, ix:ix + n, :].rearrange("p a q -> p (a q)")
                                    em.op(eng, lambda e, dst=dst, src=src: e.tensor_tensor(out=dst, in0=dst, in1=src, op=ALU.mult),
                                          rd=[E_, G_], wr=[E_])
                    for qs in range(nqs):
                        em.op("pe", lambda e, E_=E_, qs=qs, kt=kt, ki=ki, ntl=len(tiles), V_=V_: e.matmul(
                            PS[3][:, qs * 65:(qs + 1) * 65], lhsT=E_[:, qs * 128:(qs + 1) * 128], rhs=V_[:, kt, :],
                            start=(ki == 0 and qs == 0), stop=(ki == ntl - 1), skip_group_check=True), rd=[E_, V_], wr=[("nacc", qs)])
                for qs in range(nqs):
                    ep = cnt["ep"] % 2
                    cnt["ep"] += 1
                    s_, g_, ob_ = sc[ep], gt[ep], ob[ep]
                    r0 = q0 + qs * 128
                    em.dma("sp", g_[:, :], self.zg[r0:r0 + 128, 512 + h * 64:512 + (h + 1) * 64], wr=[g_])
                    em.op("dve", lambda e, s_=s_, qs=qs: e.reciprocal(out=s_[:, 0:1], in_=PS[3][:, qs * 65 + 64:qs * 65 + 65]), rd=[("nacc", qs)], wr=[s_])
                    em.op("dve", lambda e, s_=s_, qs=qs, g_=g_, ob_=ob_: e.scalar_tensor_tensor(
                        out=ob_[:, :], in0=PS[3][:, qs * 65:qs * 65 + 64], scalar=s_[:, 0:1], in1=g_[:, :], op0=ALU.mult, op1=ALU.mult),
                        rd=[("nacc", qs), s_, g_], wr=[ob_])
                    em.dma("sp", self.og[r0:r0 + 128, 512 + h * 64:512 + (h + 1) * 64], ob_[:, :], rd=[ob_], wr=[("og", r0, 1, h)])
        em.barrier()
        em.release(m0)

    def phaseD(self, l):
        em = self.em
        m0 = em.mark()
        PS = self.ps
        psb = self.psb
        last = (l == L - 1)
        identb = em.tile([128, 128], BF16, "identb")
        em.dma("pool", identb[:, :], self.ident_d[:, :], wr=[identb])
        wbr = em.tile([128, 3, 4, D], BF16, "wbr")
        wout = em.tile([128, 8, D], BF16, "wout")
        for n in range(3):
            em.dma("pool", wbr[:, n, :, :], self.w_branch[l, n].rearrange("(kc p) d -> p kc d", p=128), wr=[wbr])
        em.dma("pool", wout[:, :, :], self.w_out[l].rearrange("(k p) d -> p k d", p=128), wr=[wout])
        Gm = em.tile([128, 2, D], F32, "Gm")
        em.dma("sp", Gm[:, :, :], self.modG[l].rearrange("v p d -> p v d"), rd=[("modG", l)], wr=[Gm])
        NB = 2
        ogt = [em.tile([128, 1536], BF16, "ogt") for _ in range(NB)]
        ogT = [em.tile([128, 12, 128], BF16, "ogT") for _ in range(NB)]
        gb = [em.tile([128, 3072], BF16, "gb") for _ in range(NB)]
        acc = [em.tile([128, D], F32, "acc") for _ in range(NB)]
        tmpd = [em.tile([128, 512], F32, "tmpd") for _ in range(2)]
        accb = [em.tile([128, D], BF16, "accb") for _ in range(NB)]
        accT = [em.tile([128, 8, 128], BF16, "accT") for _ in range(NB)]
        yt = [em.tile([128, D], F32, "yt") for _ in range(NB)]
        xt = [em.tile([128, D], F32, "xt") for _ in range(NB)]
        jk = em.tile([128, D], F32, "jk")
        sc = [em.tile([128, 4], F32, "sc") for _ in range(NB)]
        cps = [0]
        tiles = list(range(2, NT)) if last else list(range(NT))
        for it, tt in enumerate(tiles):
            b_ = it % NB
            og_, ogT_, gb_, acc_, accb_, accT_, y_, x_, s_ = ogt[b_], ogT[b_], gb[b_], acc[b_], accb[b_], accT[b_], yt[b_], xt[b_], sc[b_]
            v = 1 if tt < 2 else 0
            r0 = tt * 128
            em.dma("sp", og_[:, :], self.og[r0:r0 + 128, :], wr=[og_])
            em.dma("sp", gb_[:, :], self.zmg[r0:r0 + 128, :], wr=[gb_])
            em.dma("sp", x_[:, :], self.res_src(l, tt), rd=([("xres", tt)] if l > 0 else []), wr=[x_])
            for grp, (c0, nblk) in enumerate(((0, 8), (8, 4))):
                for j in range(nblk):
                    em.op("pe", lambda e, j=j, c0=c0, og_=og_: e.transpose(
                        out=psb[:, j * 128:(j + 1) * 128], in_=og_[:, (c0 + j) * 128:(c0 + j + 1) * 128], identity=identb[:, :]),
                        rd=[og_, identb], wr=[psb])
                dst = ogT_[:, c0:c0 + nblk, :].rearrange("p a t -> p (a t)")
                if grp == 0:
                    em.op("act", lambda e, dst=dst, nblk=nblk: e.activation(out=dst, in_=psb[:, 0:nblk * 128], func=AF.Copy), rd=[psb], wr=[ogT_])
                else:
                    em.op("dve", lambda e, dst=dst, nblk=nblk: e.tensor_copy(out=dst, in_=psb[:, 0:nblk * 128]), rd=[psb], wr=[ogT_])
            for n in range(3):
                for hc in range(2):
                    ps = PS[cps[0] % 6]
                    cps[0] += 1
                    for kc in range(4):
                        em.op("pe", lambda e, ps=ps, n=n, kc=kc, hc=hc, ogT_=ogT_: e.matmul(
                            ps[:, :], lhsT=ogT_[:, n * 4 + kc, :], rhs=wbr[:, n, kc, hc * 512:(hc + 1) * 512],
                            start=(kc == 0), stop=(kc == 3)), rd=[ogT_, wbr], wr=[ps])
                    gsl = gb_[:, n * D + hc * 512:n * D + (hc + 1) * 512]
                    asl = acc_[:, hc * 512:(hc + 1) * 512]
                    if n == 0:
                        em.op("dve", lambda e, ps=ps, gsl=gsl, asl=asl: e.tensor_tensor(out=asl, in0=ps[:, :], in1=gsl, op=ALU.mult),
                              rd=[ps, gb_], wr=[acc_])
                    else:
                        td = tmpd[hc]
                        em.op("dve", lambda e, ps=ps, gsl=gsl, td=td: e.tensor_tensor(out=td[:, :], in0=ps[:, :], in1=gsl, op=ALU.mult),
                              rd=[ps, gb_], wr=[td])
                        em.op("pool", lambda e, td=td, asl=asl: e.tensor_tensor(out=asl, in0=asl, in1=td[:, :], op=ALU.add),
                              rd=[td, acc_], wr=[acc_])
            em.op("act", lambda e, acc_=acc_, accb_=accb_: e.activation(out=accb_[:, :], in_=acc_[:, :], func=AF.Copy), rd=[acc_], wr=[accb_])
            for j in range(8):
                em.op("pe", lambda e, j=j, accb_=accb_: e.transpose(
                    out=psb[:, j * 128:(j + 1) * 128], in_=accb_[:, j * 128:(j + 1) * 128], identity=identb[:, :]),
                    rd=[accb_, identb], wr=[psb])
            em.op("act", lambda e, accT_=accT_: e.activation(out=accT_[:, :, :].rearrange("p a t -> p (a t)"), in_=psb[:, :], func=AF.Copy),
                  rd=[psb], wr=[accT_])
            for hc in range(2):
                ps = PS[cps[0] % 6]
                cps[0] += 1
                for k in range(8):
                    em.op("pe", lambda e, ps=ps, k=k, hc=hc, accT_=accT_: e.matmul(
                        ps[:, :], lhsT=accT_[:, k, :], rhs=wout[:, k, hc * 512:(hc + 1) * 512], start=(k == 0), stop=(k == 7)),
                        rd=[accT_, wout], wr=[ps])
                if hc == 0:
                    em.op("act", lambda e, ps=ps, y_=y_: e.activation(out=y_[:, 0:512], in_=ps[:, :], func=AF.Copy), rd=[ps], wr=[y_])
                else:
                    em.op("dve", lambda e, ps=ps, y_=y_: e.tensor_copy(out=y_[:, 512:1024], in_=ps[:, :]), rd=[ps], wr=[y_])
            em.op("pool", lambda e, s_=s_: e.memset(s_[:, :], 0.0), wr=[s_])
            em.op("act", lambda e, y_=y_, s_=s_: e.activation(out=jk[:, :], in_=y_[:, :], func=AF.Square, accum_out=s_[:, 0:1]), rd=[y_, s_], wr=[jk, s_])
            em.op("act", lambda e, s_=s_: e.activation(out=s_[:, 1:2], in_=s_[:, 0:1], func=AF.Sqrt, bias=1e-6, scale=1.0 / D), rd=[s_], wr=[s_])
            em.op("dve", lambda e, s_=s_: e.reciprocal(out=s_[:, 2:3], in_=s_[:, 1:2]), rd=[s_], wr=[s_])
            em.op("dve", lambda e, y_=y_, s_=s_, v=v: e.scalar_tensor_tensor(
                out=y_[:, :], in0=y_[:, :], scalar=s_[:, 2:3], in1=Gm[:, v, :], op0=ALU.mult, op1=ALU.mult), rd=[y_, s_, Gm], wr=[y_])
            em.op("pool", lambda e, y_=y_, x_=x_: e.tensor_tensor(out=x_[:, :], in0=x_[:, :], in1=y_[:, :], op=ALU.add), rd=[x_, y_], wr=[x_])
            if last:
                dst = self.out[(tt - 2) * 128:(tt - 1) * 128, :]
                em.dma("sp", dst, x_[:, :], rd=[x_], wr=[("out", tt)])
            else:
                em.dma("sp", self.xres[r0:r0 + 128, :], x_[:, :], rd=[x_], wr=[("xres", tt)])
        em.barrier()
        em.release(m0)

    def build(self):
        self.declare()
        st = self.stages
        if "all" in st or "p0" in st:
            self.phase0()
        for l in range(self.nlayers):
            if "all" in st or "A" in st:
                self.phaseA(l)
            if "all" in st or "R1" in st:
                self.phaseR1(l)
            if "all" in st or "R2" in st:
                self.phaseR2(l)
            if "all" in st or "R3" in st:
                self.phaseR3(l)
            if "all" in st or "B" in st:
                self.phaseB(l)
            if "all" in st or "C" in st:
                self.phaseC(l)
            if "all" in st or "D" in st:
                self.phaseD(l)
        self.em.barrier()
        self.em.finalize()
        return self.nc


def host_consts():
    c = {}
    c["ident"] = np.eye(128, dtype=np.float32)
    rm = np.zeros((128, 128), np.float32)
    for m in range(2):
        for dd in range(64):
            a, i = dd // 32, dd % 32
            if i < 16:
                rm[m * 64 + 32 * a + 16 + i, m * 64 + dd] = -1.0
            else:
                rm[m * 64 + 32 * a + i - 16, m * 64 + dd] = 1.0
    c["rope_rm"] = rm
    ii = np.arange(64)
    sI, tI = ii[:, None], ii[None, :]
    c["tri"] = np.stack([(sI <= tI), (sI >= tI)]).astype(np.float32)
    c["mklt"] = np.stack([np.concatenate([sI < tI, sI <= tI], 1), np.concatenate([sI > tI, sI >= tI], 1)]).astype(np.float32)
    c["mkl"] = np.stack([(tI < sI), (tI > sI)]).astype(np.float32)
    t = np.arange(S, dtype=np.int32)
    row = (t // 64).astype(np.float32)
    col = (t % 64).astype(np.float32)
    inv = (np.float32(10000.0) ** (-np.arange(0, 32, 2, dtype=np.float32) / np.float32(32))).astype(np.float32)
    ar = row[:, None] * inv
    ac = col[:, None] * inv
    ang = np.concatenate([ar, ar, ac, ac], axis=-1).astype(np.float32)
    cs, sn = np.cos(ang).astype(np.float32), np.sin(ang).astype(np.float32)
    c["cosT"] = np.ascontiguousarray(np.concatenate([cs.T, cs.T], axis=0))
    sgn = np.where((np.arange(64) % 32) < 16, -1.0, 1.0).astype(np.float32)
    c["sinT"] = np.ascontiguousarray(np.concatenate([sn.T, sn.T], axis=0) * np.concatenate([sgn, sgn])[:, None])
    return c


def make_in_map(inp, b, consts):
    f = lambda a: np.ascontiguousarray(np.asarray(a, dtype=np.float32))
    m = dict(consts)
    m["xin"] = f(inp["x"][b])
    m["cin"] = f(inp["ctx"][b])
    cv = np.stack([np.asarray(inp["c"][b]), np.asarray(inp["c_ctx"])], axis=-1)
    m["cv"] = f(cv.reshape(8, 128, 2).transpose(1, 0, 2))
    m["w_mod"] = f(inp["w_mod"])
    m["b_mod"] = f(inp["b_mod"])
    m["g_pre_fm"] = f(np.asarray(inp["g_pre"]).reshape(L, 8, 128).transpose(0, 2, 1))
    m["g_post"] = f(inp["g_post"])
    m["w_in"] = f(inp["w_in"])
    m["mu"] = f(inp["shift_mu"])
    m["pvec"] = f(np.stack([np.asarray(inp["k_k"]), np.asarray(inp["k_a"]), np.asarray(inp["ln_x_g"]),
                            np.asarray(inp["ln_x_b"]), np.asarray(inp["r_k"]).reshape(L, 512)], axis=1))
    m["w0"] = f(inp["w0"])
    m["a0"] = f(inp["a0"])
    m["w_up"] = f(inp["w_up"])
    m["a_up"] = f(inp["a_up"])
    m["w_branch"] = f(inp["w_branch"])
    m["w_out"] = f(inp["w_out"])
    m["lamq"] = f(inp["lam_q"])
    m["lamk"] = f(inp["lam_k"])
    m["subln"] = f(inp["diff_subln"])
    rpb = np.asarray(inp["rpb"], dtype=np.float32)
    kc = np.arange(64)[:, None]
    qc = np.arange(64)[None, :]
    dc = np.clip(kc - qc, -15, 15) + 15
    cst = np.clip(qc - 8, 0, 48)
    ok = (kc >= cst) & (kc < cst + 16)
    tab = rpb[:, :, ::-1, :][:, :, :, dc]
    tab = np.where(ok[None, None, None], tab, np.float32(-80.0))
    m["nabias"] = f(tab.transpose(0, 1, 3, 2, 4))
    return m


_CACHE = {}


def kernel(**inputs):
    if "prog" not in _CACHE:
        P = Prog(stages=("all",))
        P.build()
        _CACHE["prog"] = P
    P = _CACHE["prog"]
    consts = host_consts()
    in_maps = []
    for b in range(8):
        im = make_in_map(inputs, b, consts)
        in_maps.append({k: v for k, v in im.items() if k in P.din})
    res = run_bass_kernel_spmd(P.nc, in_maps, core_ids=list(range(8)))
    out = np.stack([np.asarray(r["out"], dtype=np.float32) for r in res.results], axis=0)
    return out
```
